# Optimizing a Trainium2 kernel written in Bass

```python
import math
import jax, jax.numpy as jnp
from jax import lax
import numpy as np

D_MODEL = 1024
BATCH = 8
SEQ = 2048
DEPTH = 2

CHUNK = 64
Q_BLOCK = 128
N_MIXERS = 2
DA_HEAD_DIM = 64
DA_HEADS = D_MODEL // (2 * DA_HEAD_DIM)
SB_HEAD_DIM = 64
SB_HEADS = D_MODEL // SB_HEAD_DIM
D_FF = 2816
PLE_DIM = 256
N_NORMS = 8
EPS = 1e-6
NEG_BIG = -1e30

kernel_name = "hybrid_diffattn_stickbreaking_macaron_trunk"


def rmsnorm(x, g):
    xf = x.astype(jnp.float32)
    y = xf * lax.rsqrt(jnp.mean(xf * xf, axis=-1, keepdims=True) + EPS)
    return (y * g.astype(jnp.float32)).astype(x.dtype)


def swiglu(h, w_in, w_out):
    a, b = jnp.split(h @ w_in, 2, axis=-1)
    return (jax.nn.silu(a) * b) @ w_out


def alibi_slopes(n_heads):
    return jnp.asarray([2.0 ** (-8.0 * (h + 1) / n_heads) for h in range(n_heads)], dtype=jnp.float32)


def diff_attention(h, w_qkv, w_o, lam_vecs, subln_g, lambda_init):
    B, S, _ = h.shape
    H, d = DA_HEADS, DA_HEAD_DIM
    q, k, v = jnp.split(h @ w_qkv, 3, axis=-1)
    q = q.reshape(B, S, H, 2, d)
    k = k.reshape(B, S, H, 2, d)
    v = v.reshape(B, S, H, 2 * d)
    lv = lam_vecs.astype(jnp.float32)
    lam = jnp.exp(jnp.sum(lv[0] * lv[1])) - jnp.exp(jnp.sum(lv[2] * lv[3])) + lambda_init
    slopes = alibi_slopes(H)
    scale = 1.0 / math.sqrt(d)
    outs = []
    for blk in range(S // Q_BLOCK):
        q0, end = blk * Q_BLOCK, (blk + 1) * Q_BLOCK
        qb, kb, vb = q[:, q0:end], k[:, :end], v[:, :end]
        s = jnp.einsum('bqhcd,bkhcd->bchqk', qb, kb).astype(jnp.float32) * scale
        tq = jnp.arange(q0, end)
        tk = jnp.arange(end)
        dist = jnp.abs(tq[:, None] - tk[None, :]).astype(jnp.float32)
        bias = -slopes[:, None, None] * dist
        allowed = (tk // CHUNK)[None, :] <= (tq // CHUNK)[:, None]
        s = jnp.where(allowed, s + bias, NEG_BIG)
        pr = jax.nn.softmax(s, axis=-1)
        a = pr[:, 0] - lam * pr[:, 1]
        outs.append(jnp.einsum('bhqk,bkhe->bqhe', a.astype(vb.dtype), vb))
    o = jnp.concatenate(outs, axis=1)
    o = rmsnorm(o, subln_g) * (1.0 - lambda_init)
    return o.reshape(B, S, H * 2 * d) @ w_o


def stick_breaking_attention(h, w_qkv, w_o):
    B, S, _ = h.shape
    H, d = SB_HEADS, SB_HEAD_DIM
    q, k, v = jnp.split(h @ w_qkv, 3, axis=-1)
    q = q.reshape(B, S, H, d)
    k = k.reshape(B, S, H, d)
    v = v.reshape(B, S, H, d)
    scale = 1.0 / math.sqrt(d)
    outs = []
    for blk in range(S // Q_BLOCK):
        q0, end = blk * Q_BLOCK, (blk + 1) * Q_BLOCK
        qb, kb, vb = q[:, q0:end], k[:, :end], v[:, :end]
        z = jnp.einsum('bqhd,bkhd->bhqk', qb, kb).astype(jnp.float32) * scale
        tq = jnp.arange(q0, end)
        tk = jnp.arange(end)
        strict = tk[None, :] < tq[:, None]
        log_not = jnp.where(strict, jax.nn.log_sigmoid(-z), 0.0)
        suffix = lax.cumsum(log_not, axis=3, reverse=True) - log_not
        log_a = jax.nn.log_sigmoid(z) + suffix
        a = jnp.where(strict, jnp.exp(log_a), 0.0)
        outs.append(jnp.einsum('bhqk,bkhd->bqhd', a.astype(vb.dtype), vb))
    o = jnp.concatenate(outs, axis=1)
    return o.reshape(B, S, H * d) @ w_o


def setup_inputs(seed: int = 0) -> dict:
    key = jax.random.key(seed)
    ks = jax.random.split(key, 16)
    n_a = (DEPTH + 1) // 2
    n_b = DEPTH // 2
    D, F = D_MODEL, D_FF
    nrm = jax.random.normal
    x = nrm(ks[0], (BATCH, SEQ, D), jnp.float32)
    p = nrm(ks[1], (DEPTH, BATCH, SEQ, PLE_DIM), jnp.float32)
    norm_gains = 1.0 + 0.05 * nrm(ks[2], (DEPTH, N_NORMS, D), jnp.float32)
    ffn1_w_in = nrm(ks[3], (DEPTH, D, 2 * F), jnp.float32) * D ** -0.5
    ffn1_w_out = nrm(ks[4], (DEPTH, F, D), jnp.float32) * F ** -0.5
    ffn2_w_in = nrm(ks[5], (DEPTH, D, 2 * F), jnp.float32) * D ** -0.5
    ffn2_w_out = nrm(ks[6], (DEPTH, F, D), jnp.float32) * F ** -0.5
    da_w_qkv = nrm(ks[7], (n_a, D, 3 * D), jnp.float32) * D ** -0.5
    da_w_o = nrm(ks[8], (n_a, D, D), jnp.float32) * D ** -0.5
    da_lambda = 0.1 * nrm(ks[9], (n_a, 4, DA_HEAD_DIM), jnp.float32)
    da_subln = 1.0 + 0.05 * nrm(ks[10], (n_a, 2 * DA_HEAD_DIM), jnp.float32)
    sb_w_qkv = nrm(ks[11], (n_b, D, 3 * D), jnp.float32) * D ** -0.5
    sb_w_o = nrm(ks[12], (n_b, D, D), jnp.float32) * D ** -0.5
    ple_w_proj = nrm(ks[13], (DEPTH, PLE_DIM, D), jnp.float32) * PLE_DIM ** -0.5
    ple_w_gate = nrm(ks[14], (DEPTH, D, D), jnp.float32) * D ** -0.5
    return {"x": x, "p": p, "norm_gains": norm_gains,
            "ffn1_w_in": ffn1_w_in, "ffn1_w_out": ffn1_w_out,
            "ffn2_w_in": ffn2_w_in, "ffn2_w_out": ffn2_w_out,
            "da_w_qkv": da_w_qkv, "da_w_o": da_w_o, "da_lambda": da_lambda, "da_subln": da_subln,
            "sb_w_qkv": sb_w_qkv, "sb_w_o": sb_w_o,
            "ple_w_proj": ple_w_proj, "ple_w_gate": ple_w_gate}


def reference(x, p, norm_gains, ffn1_w_in, ffn1_w_out, ffn2_w_in, ffn2_w_out,
              da_w_qkv, da_w_o, da_lambda, da_subln, sb_w_qkv, sb_w_o,
              ple_w_proj, ple_w_gate):
    for i in range(DEPTH):
        g = norm_gains[i]
        x = x + 0.5 * rmsnorm(swiglu(rmsnorm(x, g[0]), ffn1_w_in[i], ffn1_w_out[i]), g[1])
        h = rmsnorm(x, g[2])
        j = i // N_MIXERS
        if i % N_MIXERS == 0:
            lambda_init = 0.8 - 0.6 * math.exp(-0.3 * i)
            m = diff_attention(h, da_w_qkv[j], da_w_o[j], da_lambda[j], da_subln[j], lambda_init)
        else:
            m = stick_breaking_attention(h, sb_w_qkv[j], sb_w_o[j])
        x = x + rmsnorm(m, g[3])
        x = x + 0.5 * rmsnorm(swiglu(rmsnorm(x, g[4]), ffn2_w_in[i], ffn2_w_out[i]), g[5])
        e = p[i] @ ple_w_proj[i]
        gate = jax.nn.sigmoid(rmsnorm(x, g[6]) @ ple_w_gate[i])
        x = x + rmsnorm(gate * e, g[7])
    return x
```

```python
import math
from contextlib import ExitStack

import numpy as np
import concourse.bass as bass
import concourse.mybir as mybir
from concourse.bass_utils import run_bass_kernel_spmd

F32 = mybir.dt.float32
BF16 = mybir.dt.bfloat16
AF = mybir.ActivationFunctionType
ALU = mybir.AluOpType

D = 1024
S_LEN = 2048
NT = 16
FF = 2816
NF = 22
EPS = 1e-6
N_CORES = 8
SAME_ENGINE_SYNC = True

_ESZ = {}


def _esz(dt):
    k = str(dt)
    if k not in _ESZ:
        _ESZ[k] = 2 if ("bfloat16" in k or "float16" in k) else (4 if "32" in k else (1 if "8" in k else 4))
    return _ESZ[k]


class Op:
    __slots__ = ("eng", "fn", "deps", "dsem", "inc", "val", "idx", "isdma")

    def __init__(self, eng, fn, dsem):
        self.eng = eng
        self.fn = fn
        self.deps = set()
        self.dsem = dsem
        self.isdma = dsem is not None
        self.inc = False
        self.val = None


class Sched:
    def __init__(self):
        self.ops = []
        self.bufs = {}
        self.log = None

    @staticmethod
    def region(ap):
        sp = str(ap.space)
        if "SB" not in sp and "PSUM" not in sp:
            return None
        pairs = ap.ap
        off = int(ap.offset)
        esz = _esz(ap.dtype)
        if "PSUM" in sp:
            pstep = pairs[0][0]
            col0 = off % pstep
            ext = 1
            for st, c in pairs[1:]:
                ext += (c - 1) * abs(st)
            lo = (col0 * esz) // 2048 * 2048
            hi = -(-((col0 + ext) * esz) // 2048) * 2048
            return (ap.name, 0, 128, lo, hi)
        pstep, pcount = pairs[0]
        if pstep == 0:
            p0, col0 = 0, off
            pcount = 1
        else:
            p0, col0 = off // pstep, off % pstep
        ext = 1
        for st, c in pairs[1:]:
            ext += (c - 1) * abs(st)
        return (ap.name, p0, p0 + pcount, col0 * esz, (col0 + ext) * esz)

    def add(self, eng, fn, r=(), w=(), dsem=None):
        op = Op(eng, fn, dsem)
        op.idx = len(self.ops)
        self.ops.append(op)
        for ap in r:
            if ap is None or isinstance(ap, (int, float)):
                continue
            rg = self.region(ap)
            if rg is None:
                continue
            name, p0, p1, lo, hi = rg
            b = self.bufs.setdefault(name, {"w": [], "r": []})
            for (q0, q1, l2, h2, o) in b["w"]:
                if q0 < p1 and p0 < q1 and l2 < hi and lo < h2:
                    op.deps.add(o)
            rl = b["r"]
            if name == "psall":
                for (q0, q1, l2, h2, o) in rl:
                    if l2 < hi and lo < h2 and self.ops[o].eng != eng:
                        op.deps.add(o)
            if not op.isdma:
                rl[:] = [x for x in rl if not (x[0] == p0 and x[1] == p1 and x[2] == lo and x[3] == hi
                                                and self.ops[x[4]].eng == eng and not self.ops[x[4]].isdma)]
            rl.append((p0, p1, lo, hi, op.idx))
        for ap in w:
            rg = self.region(ap)
            if rg is None:
                continue
            name, p0, p1, lo, hi = rg
            b = self.bufs.setdefault(name, {"w": [], "r": []})
            for key in ("w", "r"):
                keep = []
                for x in b[key]:
                    (q0, q1, l2, h2, o) = x
                    if q0 < p1 and p0 < q1 and l2 < hi and lo < h2:
                        if o != op.idx:
                            op.deps.add(o)
                        if p0 <= q0 and q1 <= p1 and lo <= l2 and h2 <= hi:
                            continue
                    keep.append(x)
                b[key] = keep
            b["w"].append((p0, p1, lo, hi, op.idx))
        return op

    def finalize(self):
        ops = self.ops
        for op in ops:
            nd = set()
            for j in op.deps:
                d = ops[j]
                if d.isdma:
                    nd.add(j)
                elif d.eng == op.eng and not op.isdma:
                    if op.eng == "pe" or not SAME_ENGINE_SYNC:
                        continue
                    nd.add(j)
                else:
                    nd.add(j)
            latest = {}
            keep = set()
            for j in nd:
                d = ops[j]
                if d.isdma:
                    keep.add(j)
                elif j > latest.get(d.eng, -1):
                    latest[d.eng] = j
            keep.update(latest.values())
            nd = keep
            op.deps = nd
            for j in nd:
                ops[j].inc = True
        cnt = {}
        for op in ops:
            if op.isdma:
                k = ("dma", op.dsem)
                cnt[k] = cnt.get(k, 0) + 16
                op.val = (k, cnt[k])
            elif op.inc:
                k = ("eng", op.eng)
                cnt[k] = cnt.get(k, 0) + 1
                op.val = (k, cnt[k])
        self.final = dict(cnt)
        return sorted(cnt.keys(), key=str)

    def emit(self, nc, sems):
        ops = self.ops
        streams = {}
        for op in ops:
            streams.setdefault(op.eng, []).append(op)
        final = self.final

        def run(engine, name, is_last):
            known = {}
            for op in streams.get(name, []):
                need = {}
                for j in op.deps:
                    k, v = ops[j].val
                    if v > need.get(k, 0):
                        need[k] = v
                for k, v in need.items():
                    if known.get(k, 0) >= v:
                        continue
                    engine.wait_ge(sems[k], v)
                    known[k] = v
                ins = op.fn(engine)
                if self.log is not None:
                    self.log.append((name, op.idx, sorted(need.items(), key=str), op.val, str(ins.concise())[:150]))
                if op.val is not None:
                    k, v = op.val
                    ins.then_inc(sems[k], 16 if op.isdma else 1)
            if is_last:
                for k, v in final.items():
                    if k[0] == "dma":
                        engine.wait_ge(sems[k], v)

        with nc.Block() as block:
            @block.sync
            def _(e):
                run(e, "sp", True)

            @block.gpsimd
            def _(e):
                run(e, "pool", False)

            @block.tensor
            def _(e):
                run(e, "pe", False)

            @block.scalar
            def _(e):
                run(e, "act", False)

            @block.vector
            def _(e):
                run(e, "dve", False)


def _const_tables():
    k = np.arange(128, dtype=np.float64)[:, None]
    q = np.arange(128, dtype=np.float64)[None, :]
    slopes = np.array([2.0 ** (-8.0 * (h + 1) / 8) for h in range(8)])
    bt = np.zeros((128, 8, 16), np.float32)
    for h in range(8):
        for dl in range(16):
            bt[:, h, dl] = slopes[h] * (np.arange(128) - 128.0 * dl - 64.0)
    bd = np.zeros((128, 8, 128), np.float32)
    allowed = (np.floor(k / 64) <= np.floor(q / 64))
    for h in range(8):
        v = -slopes[h] * np.abs(q - k) + slopes[h] * (q - 64.0)
        bd[:, h, :] = np.where(allowed, v, -30000.0)
    m01 = (k < q).astype(np.float32)
    cb = np.zeros((128, 4, 128), np.float32)
    cb[:, 0, :] = np.eye(128)
    cb[:, 1, :] = -(k >= q).astype(np.float32)
    cb[:, 2, :] = -1.0
    cb[:, 3, :] = np.where(k < q, 0.0, -30000.0)
    cf = np.concatenate([bt.reshape(128, 128), m01], axis=1)
    return cf.astype(np.float32), bd.reshape(128, 1024), cb.reshape(128, 512)


def _lay_win(w):
    a = w[:, :FF].reshape(8, 128, NF, 128)
    b = w[:, FF:].reshape(8, 128, NF, 128)
    ab = np.stack([a, b], axis=3)
    return np.ascontiguousarray(ab.transpose(2, 1, 0, 3, 4)).reshape(NF, 128, 8 * 256)


def _lay_qkv(w):
    r = w.reshape(8, 128, 3, 8, 128)
    return np.ascontiguousarray(r.transpose(3, 1, 0, 2, 4)).reshape(8, 128, 8 * 384)


O_WS = 0
O_CB = 6144
O_XN = 6656
O_HT = 8704
O_UT = 16896
O_WOUT = 28160
O_WOA = 41472
O_WOB = 28160
O_SA = 31232
O_JK = 51712
O_OT = 25088
O_QK = 41472
O_V = 53760
O_MISC = 57920
NAB = 61248
F_GN = 0
F_TMP = 2048
F_BD = 3072
F_GE = 4096
F_PF = 4608
F_CF = 5120
F_SG = 5376
F_LAM = 5504
F_ST = 5760
F_JK = 6016
F_PERS = 6144
NAF = 6160


def build_program(stages=None, depth=2, dbg=None):
    dbg = dbg or {}
    if stages is None:
        stages = ["ffn1", "attn", "wo", "ffn2", "ple"]
    nc = bass.Bass("TRN2", target_bir_lowering=False)
    S = Sched()
    dr = {}

    def dram(name, shape, kind="ExternalInput"):
        dr[name] = nc.dram_tensor(name, list(shape), F32, kind=kind).ap()
        return dr[name]

    x_d = dram("x", [S_LEN, D])
    p_d = dram("p", [2, S_LEN, 256])
    g_d = dram("gains", [2, 8, D])
    win_d = dram("win", [4, NF, 128, 2048])
    wout_d = dram("wout", [4, FF, D])
    wqkv_d = dram("wqkv", [2, 8, 128, 3072])
    wo_d = dram("wo", [2, D, D])
    wg_d = dram("wg", [2, D, D])
    wp_d = dram("wp", [2, 256, D])
    lam_d = dram("lam", [256])
    sub_d = dram("subln", [128])
    cf_d = dram("cf", [128, 256])
    bd_d = dram("bd", [128, 1024])
    cb_d = dram("cb", [128, 512])
    out_d = dram("out", [S_LEN, D], kind="ExternalOutput")

    X = nc.alloc_sbuf_tensor("X", [128, NT, D], F32)
    AB = nc.alloc_sbuf_tensor("AB", [128, NAB], BF16)
    AFt = nc.alloc_sbuf_tensor("AFt", [128, NAF], F32)
    PSALL = nc.alloc_psum_tensor("psall", [128, 4096], F32)
    PSD = [PSALL[:, m * 1024:(m + 1) * 1024].rearrange("p (b n) -> p b n", b=2) for m in range(4)]
    PS = [PSALL[:, i * 512:(i + 1) * 512] for i in range(8)]
    PSB = [PS[i].bitcast(BF16) for i in range(8)]

    def ab(o, n):
        return AB[:, o:o + n]

    def af(o, n):
        return AFt[:, o:o + n]

    IDENT = ab(O_CB, 128)
    NEGTRI = ab(O_CB + 128, 128)
    NEGONE = ab(O_CB + 256, 128)
    MDIAG = ab(O_CB + 384, 128)
    HTv = ab(O_HT, 16384).rearrange("p (t c k) -> p t c k", t=16, c=8)
    JKB = ab(O_JK, 512)
    JKF = af(F_JK, 128)
    BT = af(F_CF, 128)
    M01 = af(F_CF + 128, 128)
    SG = af(F_SG, 128)
    LAMB = af(F_LAM, 256)
    NEGHALF = af(F_PERS + 1, 1)

    cnt = {"st": 0, "pt": 0, "ws": 0, "gn": 0}

    def mm(out, lhsT, rhs, start, stop, skip=False):
        def fn(e, out=out, lhsT=lhsT, rhs=rhs, start=start, stop=stop, skip=skip):
            if skip:
                return e.matmul(out, lhsT, rhs, start=start, stop=stop, skip_group_check=True)
            return e.matmul(out, lhsT, rhs, start=start, stop=stop)
        S.add("pe", fn, r=[lhsT, rhs], w=[out])

    def tr(out, in_):
        def fn(e, out=out, in_=in_):
            return e.transpose(out, in_, IDENT)
        S.add("pe", fn, r=[in_, IDENT], w=[out])

    def act(out, in_, func, bias=None, scale=1.0, accum=None, junk=False):
        def fn(e, out=out, in_=in_, func=func, bias=bias, scale=scale, accum=accum):
            kw = {}
            if bias is not None:
                kw["bias"] = bias
            if accum is not None:
                kw["accum_out"] = accum
            return e.activation(out, in_, func, scale=scale, **kw)
        r = [in_]
        if bias is not None and not isinstance(bias, float):
            r.append(bias)
        w = [] if junk else [out]
        if accum is not None:
            w.append(accum)
        S.add("act", fn, r=r, w=w)

    def ts(out, in0, s1, s2, op0, op1=None, eng="dve"):
        def fn(e, out=out, in0=in0, s1=s1, s2=s2, op0=op0, op1=op1):
            if op1 is None:
                return e.tensor_scalar(out, in0, s1, None, op0)
            return e.tensor_scalar(out, in0, s1, s2, op0, op1)
        r = [in0] + [a for a in (s1, s2) if a is not None and not isinstance(a, (int, float))]
        S.add(eng, fn, r=r, w=[out])

    def stt(out, in0, scalar, in1, op0, op1, accum=None, junk=False):
        def fn(e, out=out, in0=in0, scalar=scalar, in1=in1, op0=op0, op1=op1, accum=accum):
            if accum is not None:
                return e.scalar_tensor_tensor(out, in0, scalar, in1, op0, op1, accum_out=accum)
            return e.scalar_tensor_tensor(out, in0, scalar, in1, op0, op1)
        r = [in0, in1] + ([scalar] if not isinstance(scalar, (int, float)) else [])
        w = [] if junk else [out]
        if accum is not None:
            w.append(accum)
        S.add("dve", fn, r=r, w=w)

    def tt(out, in0, in1, op, eng="dve"):
        def fn(e, out=out, in0=in0, in1=in1, op=op):
            return e.tensor_tensor(out, in0, in1, op)
        S.add(eng, fn, r=[in0, in1], w=[out])

    def rstd(rs, ss, epsv):
        t1 = stcol()
        ts(t1, ss, epsv, None, ALU.add, eng="pool")
        tt(rs, t1, NEGHALF, ALU.pow, eng="pool")

    def cp(out, in_, eng="dve"):
        if eng == "act":
            def fn(e, out=out, in_=in_):
                return e.activation(out, in_, AF.Copy)
        else:
            def fn(e, out=out, in_=in_):
                return e.tensor_copy(out, in_)
        S.add(eng, fn, r=[in_], w=[out])

    def recip(out, in_):
        def fn(e, out=out, in_=in_):
            return e.reciprocal(out, in_)
        S.add("dve", fn, r=[in_], w=[out])

    def memset(ap, v):
        def fn(e, ap=ap, v=v):
            return e.memset(ap, v)
        S.add("dve", fn, r=[], w=[ap])

    def dma(q, out, in_, key):
        def fn(e, out=out, in_=in_):
            return e.dma_start(out=out, in_=in_)
        S.add(q, fn, r=[in_], w=[out], dsem=key)

    def stcol():
        c = cnt["st"] % 256
        cnt["st"] += 1
        return af(F_ST + c, 1)

    def ptbank(banks=(6, 7)):
        c = cnt["pt"] % len(banks)
        cnt["pt"] += 1
        return PSB[banks[c]]

    cpeng = {"i": 0}

    def cp_alt(out, in_):
        cpeng["i"] += 1
        cp(out, in_, eng=("act" if cpeng["i"] % 2 else "dve"))

    def load_gain(l, k):
        slot = cnt["gn"] % 2
        cnt["gn"] += 1
        g = af(F_GN + slot * 1024, 1024)
        dma("sp", g, g_d[l, k, :].partition_broadcast(128), ("gn", slot))
        return g

    def prenorm_a(t, gain):
        xn = ab(O_XN + (t % 2) * 1024, 1024)
        ss = stcol()
        rs = stcol()
        act(xn, X[:, t, :], AF.Square, scale=1.0 / 32.0, accum=ss, junk=False)
        rstd(rs, ss, EPS)
        stt(xn, X[:, t, :], rs, gain, ALU.mult, ALU.mult)

    def prenorm_b(t, ht_tile, banks=(6, 7)):
        xn = ab(O_XN + (t % 2) * 1024, 1024)
        pt = ptbank(banks)
        for cc in range(8):
            tr(pt[:, cc * 128:(cc + 1) * 128], xn[:, cc * 128:(cc + 1) * 128])
        cp_alt(ab(O_HT + ht_tile * 1024, 1024), pt)

    def prenorm(t, gain, ht_tile):
        prenorm_a(t, gain)
        prenorm_b(t, ht_tile)

    def prenorm_seq(tiles, gain, slots):
        prenorm_a(tiles[0], gain)
        for idx, t in enumerate(tiles):
            if idx + 1 < len(tiles):
                prenorm_a(tiles[idx + 1], gain)
            prenorm_b(t, slots[idx])

    def postnorm(t, ys, gain, factor, inplace=False):
        sc = (2.0 if factor == 0.5 else 1.0) / 32.0
        epsv = EPS * (4.0 if factor == 0.5 else 1.0)
        s0, s1, s2, rs = stcol(), stcol(), stcol(), stcol()
        tmps = [af(F_TMP + hh * 512, 512) for hh in range(2)]
        if inplace:
            jb = AFt[:, F_GE:F_GE + 512].bitcast(BF16)
            jk = [jb[:, 0:512], jb[:, 512:1024]]
        else:
            jk = tmps
        act(jk[0], ys[0], AF.Square, scale=sc, accum=s0)
        act(jk[1], ys[1], AF.Square, scale=sc, accum=s1)
        tt(s2, s0, s1, ALU.add)
        rstd(rs, s2, epsv)
        for hh in range(2):
            tmp = tmps[hh]
            stt(tmp, ys[hh], rs, gain[:, hh * 512:(hh + 1) * 512], ALU.mult, ALU.mult)
            xs = X[:, t, hh * 512:(hh + 1) * 512]
            tt(xs, xs, tmp, ALU.add)

    prefetched = set()

    def wout_a_dma(wi):
        src = wout_d[wi].rearrange("(f p) d -> p f d", p=128)
        WOAv = ab(O_WOA, 19 * 1024).rearrange("p (f d) -> p f d", f=19)
        dma("pool", WOAv[:, 0:10, :], src[:, 0:10, :], ("wout", 0))
        dma("pool", WOAv[:, 10:19, :], src[:, 10:19, :], ("wout", 1))
        prefetched.add(wi)

    def ffn(l, which):
        wi = l * 2 + which
        gpre = load_gain(l, 0 if which == 0 else 4)
        gpost = load_gain(l, 1 if which == 0 else 5)
        UT = ab(O_UT, NF * 512).rearrange("p (f t) -> p f t", f=NF)
        wout_src = wout_d[wi].rearrange("(f p) d -> p f d", p=128)

        def wout(f):
            if f < 19:
                return ab(O_WOA + f * 1024, 1024)
            return ab(O_WOB + (f - 19) * 1024, 1024)
        prenorm_seq([0, 1, 2, 3], gpre, [0, 1, 2, 3])

        def win_dma(sidx):
            if sidx >= 4 * NF:
                return
            f = sidx % NF
            wt = ab(O_WS + (sidx % 3) * 2048, 2048)
            dma("pool", wt, win_d[wi, f], ("ws", sidx % 3))

        for sidx in range(3):
            win_dma(sidx)
        if wi not in prefetched:
            wout_a_dma(wi)
        WOBv = ab(O_WOB, 3 * 1024).rearrange("p (f d) -> p f d", f=3)
        dma("pool", WOBv, wout_src[:, 19:22, :], ("wout", 2))
        for g in range(4):
            hb = (g % 2) * 4
            for f in range(NF):
                sidx = g * NF + f
                wt = ab(O_WS + (sidx % 3) * 2048, 2048)
                wv = wt.rearrange("p (c n) -> p c n", c=8)
                A = PS[2 * (f % 2)]
                B = PS[2 * (f % 2) + 1]
                Av = A.rearrange("p (t k) -> p t k", t=4)
                Bv = B.rearrange("p (t k) -> p t k", t=4)
                for c in range(8):
                    mm(Av, wv[:, c, 0:128], HTv[:, hb:hb + 4, c, :], c == 0, c == 7)
                for c in range(8):
                    mm(Bv, wv[:, c, 128:256], HTv[:, hb:hb + 4, c, :], c == 0, c == 7)
                win_dma(sidx + 3)
                sa = ab(O_SA + (f % 2) * 512, 512)
                act(sa, A, AF.Silu)
                tt(UT[:, f, :], sa, B, ALU.mult)
            for tt_ in range(4):
                t = 4 * g + tt_
                if g < 3:
                    prenorm_a(4 * (g + 1) + tt_, gpre)
                yb = [(4, 5), (6, 7)][tt_ % 2]
                ys = [PS[yb[0]], PS[yb[1]]]
                for hh in range(2):
                    for f in range(NF):
                        mm(ys[hh], UT[:, f, tt_ * 128:(tt_ + 1) * 128], wout(f)[:, hh * 512:(hh + 1) * 512],
                           f == 0, f == NF - 1)
                postnorm(t, ys, gpost, 0.5)
                if g < 3 and not dbg.get("ffn_nopn"):
                    prenorm_b(4 * (g + 1) + tt_, ((g + 1) % 2) * 4 + tt_, banks=(0, 1))
            if g < 3 and dbg.get("ffn_nopn"):
                for k2 in range(4):
                    prenorm_b(4 * (g + 1) + k2, ((g + 1) % 2) * 4 + k2)

    QKv = ab(O_QK, 12288).rearrange("p (s w n) -> p s w n", s=2, w=3)
    Vv = ab(O_V, 4160).rearrange("p (s t e) -> p s t e", s=2, t=16)
    OTv = ab(O_OT, 16384).rearrange("p (h n) -> p h n", h=8)

    def project(mi, h, bank):
        slot = h % 2
        wt = ab(O_WS + slot * 3072, 3072)
        wv = wt.rearrange("p (c n) -> p c n", c=8)
        dma("pool", wv, wqkv_d[mi, h].rearrange("p (c n) -> p c n", c=8), ("ws", slot))
        QT0 = QKv[:, slot, 0, :]
        QT1 = QKv[:, slot, 1, :]
        KT = QKv[:, slot, 2, :]
        QT = (QT0, QT1)
        P6 = PS[bank]
        P6v = P6.rearrange("p (t k) -> p t k", t=4)
        chunks = []

        def cq(n, half):
            for c in range(4 * half, 4 * half + 4):
                mm(P6v, wv[:, c, 0:128], HTv[:, 4 * n:4 * n + 4, c, :], c == 0, c == 7)
            if half == 1:
                ts(QT0[0:64, n * 512:(n + 1) * 512], P6[0:64, :], 0.125, None, ALU.mult)
                ts(QT1[64:128, n * 512:(n + 1) * 512], P6[64:128, :], 0.125, None, ALU.mult)

        def ck(n, half):
            for c in range(4 * half, 4 * half + 4):
                mm(P6v, wv[:, c, 128:256], HTv[:, 4 * n:4 * n + 4, c, :], c == 0, c == 7)
            if half == 1:
                cp(KT[:, n * 512:(n + 1) * 512], P6, eng="dve")

        def cv(n, k):
            t = 4 * n + k
            for c in range(8):
                mm(P6[:, k * 128:(k + 1) * 128], HTv[:, t, c, :], wv[:, c, 256:384], c == 0, c == 7)
            if k == 3:
                cp(Vv[:, slot, 4 * n:4 * n + 4, 0:128], P6v, eng="dve")

        for n in range(4):
            for half in range(2):
                chunks.append(lambda n=n, half=half: cq(n, half))
            for half in range(2):
                chunks.append(lambda n=n, half=half: ck(n, half))
        for n in range(4):
            for k in range(4):
                chunks.append(lambda n=n, k=k: cv(n, k))
        return QT, KT, Vv[:, slot], chunks

    def zero_q_halves():
        for sl in range(2):
            memset(QKv[64:128, sl, 0, :], 0.0)
            memset(QKv[0:64, sl, 1, :], 0.0)

    def finish_tile(on, h, j):
        pt = PSB[7][:, 0:128]
        tr(pt, on)
        cp_alt(OTv[:, h, j * 128:(j + 1) * 128], pt)

    def attn_da(l):
        lambda_init = 0.8 - 0.6 * math.exp(-0.3 * l)
        g2 = load_gain(l, 2)
        prenorm_seq(list(range(NT)), g2, list(range(NT)))
        l0, l1 = stcol(), stcol()
        stt(JKF[:, 0:64], LAMB[:, 0:64], 1.0, LAMB[:, 64:128], ALU.mult, ALU.mult, accum=l0)
        stt(JKF[:, 64:128], LAMB[:, 128:192], 1.0, LAMB[:, 192:256], ALU.mult, ALU.mult, accum=l1)
        e0, e1, dd = stcol(), stcol(), stcol()
        nlam = af(F_PERS, 1)
        act(e0, l0, AF.Exp)
        act(e1, l1, AF.Exp)
        tt(dd, e1, e0, ALU.subtract)
        ts(nlam, dd, -lambda_init, None, ALU.add)
        ts(SG, SG, 1.0 - lambda_init, None, ALU.mult)
        for s in range(2):
            memset(Vv[:, s, :, 128:129], 1.0)
        zero_q_halves()
        NPT = 8
        PTr = ab(O_MISC, NPT * 256).rearrange("p (s n) -> p s n", s=NPT)
        ONr = ab(O_MISC + NPT * 256, 256).rearrange("p (s n) -> p s n", s=2)
        dc = dbg.get("da_cut", 9)
        if dc < 1:
            return
        LAG = 4
        NH = dbg.get("da_heads", 8)
        projs = {0: project(0, 0, 6)}
        for h in range(NH):
            QT, KT, V, chs = projs[h]
            while chs:
                chs.pop(0)()
            if h == NH - 1:
                load_wo(l)
            nxt = []
            if h + 1 < NH:
                projs[h + 1] = project(0, h + 1, 6)
                nxt = projs[h + 1][3]
            if dc < 2:
                continue
            pairs = [(j, i) for j in range(dbg.get("da_nt", NT)) for i in range(j + 1)]
            n = len(pairs)

            def sbank(k):
                gi_, ix_ = k // 4, k % 4
                return PS[2 * (gi_ % 2) + ix_ // 2][:, (ix_ % 2) * 256:(ix_ % 2) * 256 + 256]

            def score_mm(k, QT=QT, KT=KT):
                j, i = pairs[k]
                Sb = sbank(k)
                for c in range(2):
                    mm(Sb[:, c * 128:(c + 1) * 128], KT[:, i * 128:(i + 1) * 128],
                       QT[c][:, j * 128:(j + 1) * 128], True, True)

            def score_diag(k, h=h):
                j, i = pairs[k]
                if i != j:
                    return
                Sb = sbank(k)
                dt_ = af(F_TMP + (j % 4) * 256, 256)
                for c in range(2):
                    tt(dt_[:, c * 128:(c + 1) * 128], Sb[:, c * 128:(c + 1) * 128],
                       af(F_BD + h * 128, 128), ALU.add)

            def score_act(k, h=h):
                j, i = pairs[k]
                Sb = sbank(k)
                pts = PTr[:, k % NPT, :]
                if i == j:
                    act(pts, af(F_TMP + (j % 4) * 256, 256), AF.Exp)
                else:
                    act(pts, Sb, AF.Exp, bias=af(F_CF + h * 16 + (j - i), 1))

            def av(k, h=h, V=V):
                j, i = pairs[k]
                accb = PS[4 + (j % 2)]
                pts = PTr[:, k % NPT, :]
                for c in range(2):
                    mm(accb[:, c * 129:(c + 1) * 129], pts[:, c * 128:(c + 1) * 128], V[:, i, 0:129],
                       i == 0 and c == 0, i == j, skip=True)
                if i != j:
                    return
                r0, r1, r2, ssq, rs2 = stcol(), stcol(), stcol(), stcol(), stcol()
                recip(r0, accb[:, 128:129])
                recip(r1, accb[:, 257:258])
                tt(r2, r1, nlam, ALU.mult)
                o1 = af(F_GE, 128)
                o2 = af(F_GE + 128 + (j % 3) * 128, 128)
                ts(o1, accb[:, 0:128], r0, None, ALU.mult)
                stt(o2, accb[:, 129:257], r2, o1, ALU.mult, ALU.add)
                stt(ONr[:, j % 2, :], o2, 1.0 / 128.0, o2, ALU.mult, ALU.mult, accum=ssq)
                rstd(rs2, ssq, EPS)

                def tail(o2=o2, rs2=rs2, j=j, h=h):
                    on = ONr[:, j % 2, :]
                    stt(on, o2, rs2, SG, ALU.mult, ALU.mult)
                    finish_tile(on, h, j)
                deferred.append((k + 8, tail))

            deferred = []
            GS = 4
            groups = [list(range(a, min(a + GS, n))) for a in range(0, n, GS)]
            NG = len(groups)
            for k in groups[0]:
                score_mm(k)
            for k in groups[0]:
                score_diag(k)
            for gi in range(NG + 1):
                if gi + 1 < NG:
                    for k in groups[gi + 1]:
                        score_mm(k)
                    for k in groups[gi + 1]:
                        score_diag(k)
                if gi < NG:
                    for k in groups[gi]:
                        score_act(k)
                kk = gi * GS
                while deferred and deferred[0][0] <= kk:
                    deferred.pop(0)[1]()
                if dc >= 4 and gi >= 1:
                    for k in groups[gi - 1]:
                        av(k)
                if nxt:
                    nxt.pop(0)()
            while deferred:
                deferred.pop(0)[1]()

    def attn_sb(l):
        g2 = load_gain(l, 2)
        prenorm_seq(list(range(NT)), g2, list(range(NT)))
        zero_q_halves()
        Lbufs = [ab(O_XN, 1024), ab(O_XN + 1024, 1024), AFt[:, F_GE:F_GE + 512].bitcast(BF16)]
        Lbufs = [x.rearrange("p (h n) -> p h n", h=2) for x in Lbufs]
        ATbufs = [ab(O_MISC + k * 1024, 1024).rearrange("p (h n) -> p h n", h=2) for k in range(3)]
        LCall = AFt[:, F_TMP:F_TMP + 2048].bitcast(BF16)
        LCbufs = [LCall[:, k * 1024:(k + 1) * 1024].rearrange("p (h n) -> p h n", h=2) for k in range(4)]
        Wt = PSD[2]
        ACC = PS[6]
        steps = []
        for hp in range(dbg.get("sb_pairs", 8)):
            for g in range(4):
                for i in range(4 * g + 3, -1, -1):
                    steps.append((hp, g, i))
        ctx = {}
        info = {}

        NP = dbg.get("sb_pairs", 8)

        def ensure_proj(hp):
            if (hp, "proj") not in ctx:
                ctx[(hp, "proj")] = project(1, hp, 7)
            chs = ctx[(hp, "proj")][3]
            while chs:
                chs.pop(0)()
            if hp == NP - 1 and "wo" not in ctx:
                ctx["wo"] = True
                load_wo(l)

        def stage_a(n):
            hp, g, i = steps[n]
            ensure_proj(hp)
            QT, KT, V, _ = ctx[(hp, "proj")]
            r = max(0, i - 4 * g)
            c0 = r * 128
            q0 = g * 512 + c0
            q1 = (g + 1) * 512
            diag = i >= 4 * g
            first = (i == 4 * g + 3)
            Zt = PSD[n % 2]
            for hd in range(2):
                mm(Zt[:, hd, c0:512], KT[:, i * 128:(i + 1) * 128], QT[hd][:, q0:q1], True, True)
            act(Zt[:, :, c0:512], Zt[:, :, c0:512], AF.Exp)
            if diag:
                for hd in range(2):
                    tt(Zt[:, hd, c0:c0 + 128], Zt[:, hd, c0:c0 + 128], M01, ALU.mult)
            info[n] = dict(c0=c0, q0=q0, q1=q1, diag=diag, first=first, QT=QT, KT=KT, V=V, Zt=Zt, i=i)

        def stage_a2(n):
            d = info[n]
            c0, first, i, Zt = d["c0"], d["first"], d["i"], d["Zt"]
            L = Lbufs[n % 3]
            act(L[:, :, c0:512], Zt[:, :, c0:512], AF.Ln, bias=1.0)
            lcprev = None if first else info[n - 1]["lcnew"]
            lcnew = None
            if i > 0:
                lcnew = LCbufs[n % 4]
                if lcprev is None:
                    cp(lcnew[:, :, c0:512], L[:, :, c0:512], eng="dve")
                else:
                    tt(lcnew[:, :, c0:512], lcprev[:, :, c0:512], L[:, :, c0:512], ALU.add)
                if c0 > 0:
                    memset(lcnew[:, :, 0:c0], 0.0)
            d.update(lcprev=lcprev, lcnew=lcnew, L=L)

        def stage_bw(n):
            hp, g, i = steps[n]
            d = info[n]
            c0 = d["c0"]
            for hd in range(2):
                seq = [(d["KT"][:, i * 128:(i + 1) * 128], d["QT"][hd][:, d["q0"]:d["q1"]], Wt[:, hd, c0:512]),
                       (NEGTRI, d["L"][:, hd, c0:512], Wt[:, hd, c0:512])]
                if d["lcprev"] is not None:
                    seq.append((NEGONE, d["lcprev"][:, hd, c0:512], Wt[:, hd, c0:512]))
                if d["diag"]:
                    seq.append((IDENT, MDIAG, Wt[:, hd, c0:c0 + 128]))
                for k, (lt, rh, o) in enumerate(seq):
                    mm(o, lt, rh, k == 0, k == len(seq) - 1, skip=True)

        def stage_bexp(n):
            d = info[n]
            c0 = d["c0"]
            AT = ATbufs[n % 3]
            act(AT[:, :, c0:512], Wt[:, :, c0:512], AF.Exp)

        def stage_bav(n):
            hp, g, i = steps[n]
            d = info[n]
            c0 = d["c0"]
            AT = ATbufs[n % 3]
            for hd in range(2):
                mm(ACC[hd * 64:(hd + 1) * 64, c0:512], d["V"][:, i, hd * 64:(hd + 1) * 64], AT[:, hd, c0:512],
                   d["first"], i == 0, skip=True)
            if i == 0:
                cp(OTv[:, hp, g * 512:(g + 1) * 512], ACC, eng="dve")
            del info[n]["L"]

        N = len(steps)

        def a1(k):
            if 0 <= k < N:
                stage_a(k)

        def a2(k):
            if 0 <= k < N:
                stage_a2(k)

        a1(0)
        a2(0)
        a1(1)
        a2(1)
        loc = 0
        for n in range(N + 1):
            if n < N:
                hp = steps[n][0]
                if n == 0 or steps[n - 1][0] != hp:
                    loc = 0
                    if hp + 1 < NP:
                        ctx[(hp + 1, "proj")] = project(1, hp + 1, 7)
                a1(n + 2)
                stage_bw(n)
                stage_bexp(n)
                a2(n + 2)
            if n >= 1:
                stage_bav(n - 1)
            if n < N:
                if hp + 1 < NP and ctx[(hp + 1, "proj")][3]:
                    ctx[(hp + 1, "proj")][3].pop(0)()
                loc += 1

    WOv = ab(O_HT, 8192).rearrange("p (c d) -> p c d", c=8)

    def load_wo(l):
        src = wo_d[l].rearrange("(c p) d -> p c d", p=128)
        dma("pool", WOv[:, 0:4, :], src[:, 0:4, :], ("wo", 0))
        dma("pool", WOv[:, 4:8, :], src[:, 4:8, :], ("wo", 1))

    def wo_phase(l):
        g3 = load_gain(l, 3)
        if "ffn2" in stages:
            wout_a_dma(l * 2 + 1)
        for t in range(NT):
            yb = [(0, 1), (2, 3)][t % 2]
            ys = [PS[yb[0]][:, :], PS[yb[1]][:, :]]
            for hh in range(2):
                for c in range(8):
                    mm(ys[hh], OTv[:, c, t * 128:(t + 1) * 128], WOv[:, c, hh * 512:(hh + 1) * 512], c == 0, c == 7)
            postnorm(t, ys, g3, 1.0)

    def ple(l, store):
        WG = ab(O_OT, 8192).rearrange("p (c d) -> p c d", c=8)
        WP = ab(O_OT + 8192, 2048).rearrange("p (c d) -> p c d", c=2)
        sg = wg_d[l].rearrange("(c p) d -> p c d", p=128)
        dma("pool", WG[:, 0:4, :], sg[:, 0:4, :], ("wg", 0))
        dma("pool", WG[:, 4:8, :], sg[:, 4:8, :], ("wg", 1))
        dma("pool", WP, wp_d[l].rearrange("(c p) d -> p c d", p=128), ("wp", 0))
        g6 = load_gain(l, 6)
        g7 = load_gain(l, 7)
        if l + 1 < depth and "ffn1" in stages:
            wout_a_dma((l + 1) * 2)
        PBr = ab(O_OT + 10240, 1024).rearrange("p (s n) -> p s n", s=4)
        PTTr = ab(O_OT + 11264, 512).rearrange("p (s n) -> p s n", s=2)

        def banks(t):
            return ((0, 1) if t % 2 == 0 else (4, 5)), (2, 3)

        def pload(t):
            if t < NT:
                dma("pool", PBr[:, t % 4, :], p_d[l, t * 128:(t + 1) * 128, :], ("pf", t % 4))

        def front(t):
            pb = PBr[:, t % 4, :]
            pt = ptbank((6, 7))
            for c2 in range(2):
                tr(pt[:, c2 * 128:(c2 + 1) * 128], pb[:, c2 * 128:(c2 + 1) * 128])
            cp(PTTr[:, t % 2, :], pt[:, 0:256], eng="dve")

        def matmuls(t, hslot):
            Gb, Eb = banks(t)
            ptt = PTTr[:, t % 2, :]
            for hh in range(2):
                Gt = PS[Gb[hh]]
                Et = PS[Eb[hh]]
                for c in range(8):
                    mm(Gt, HTv[:, hslot, c, :], WG[:, c, hh * 512:(hh + 1) * 512], c == 0, c == 7)
                for c2 in range(2):
                    mm(Et, ptt[:, c2 * 128:(c2 + 1) * 128], WP[:, c2, hh * 512:(hh + 1) * 512], c2 == 0, c2 == 1)

        def elementwise(t):
            Gb, Eb = banks(t)
            ges = []
            for hh in range(2):
                gs = af(F_BD + hh * 512, 512)
                act(gs, PS[Gb[hh]], AF.Sigmoid)
                ge = af(F_TMP + hh * 512, 512)
                tt(ge, gs, PS[Eb[hh]], ALU.mult)
                ges.append(ge)
            postnorm(t, ges, g7, 1.0, inplace=True)
            if store:
                dma("sp", out_d[t * 128:(t + 1) * 128, :], X[:, t, :], ("x", t))

        for t in range(3):
            pload(t)
        prenorm_seq([0, 1, 2, 3], g6, [0, 1, 2, 3])
        front(0)
        for t in range(NT):
            g, k = t // 4, t % 4
            pload(t + 3)
            if g < 3:
                prenorm_a(4 * (g + 1) + k, g6)
            matmuls(t, (g % 2) * 4 + k)
            if g < 3:
                prenorm_b(4 * (g + 1) + k, ((g + 1) % 2) * 4 + k)
            if t + 1 < NT:
                front(t + 1)
            elementwise(t)

    for t in range(NT):
        dma("sp", X[:, t, :], x_d[t * 128:(t + 1) * 128, :], ("x", t))
    S.add("pool", lambda e: e.memset(NEGHALF, -0.5), r=[], w=[NEGHALF])
    dma("pool", ab(O_CB, 512), cb_d, ("cst", 0))
    dma("sp", af(F_CF, 256), cf_d, ("cst", 1))
    dma("sp", af(F_BD, 1024), bd_d, ("cst", 2))
    dma("sp", LAMB, lam_d.partition_broadcast(128), ("cst", 3))
    dma("sp", SG, sub_d.partition_broadcast(128), ("cst", 4))

    if stages is None:
        stages = ["ffn1", "attn", "wo", "ffn2", "ple"]
    last_stage_store = False
    for l in range(depth):
        final_layer = (l == depth - 1)
        if "ffn1" in stages:
            ffn(l, 0)
        if "attn" in stages:
            if l == 0:
                attn_da(l)
            else:
                attn_sb(l)
            if "wo" in stages:
                wo_phase(l)
        if "ffn2" in stages:
            ffn(l, 1)
        if "ple" in stages:
            ple(l, store=final_layer)
            last_stage_store = final_layer
    if not last_stage_store:
        for t in range(NT):
            dma("sp", out_d[t * 128:(t + 1) * 128, :], X[:, t, :], ("x", t))

    keys = S.finalize()
    if dbg.get('log'):
        S.log = []
        nc._sched = S
    with ExitStack() as es:
        sems = {}
        for i, k in enumerate(keys):
            sems[k] = es.enter_context(nc.semaphore("s%d" % i))
        S.emit(nc, sems)
    return nc


def prepare_inputs(x, p, norm_gains, ffn1_w_in, ffn1_w_out, ffn2_w_in, ffn2_w_out,
                   da_w_qkv, da_w_o, da_lambda, da_subln, sb_w_qkv, sb_w_o, ple_w_proj, ple_w_gate):
    f = lambda a: np.ascontiguousarray(np.asarray(a, dtype=np.float32))
    x, p = f(x), f(p)
    win = np.stack([_lay_win(f(ffn1_w_in)[0]), _lay_win(f(ffn2_w_in)[0]),
                    _lay_win(f(ffn1_w_in)[1]), _lay_win(f(ffn2_w_in)[1])], axis=0)
    wout = np.stack([f(ffn1_w_out)[0], f(ffn2_w_out)[0], f(ffn1_w_out)[1], f(ffn2_w_out)[1]], axis=0)
    wqkv = np.stack([_lay_qkv(f(da_w_qkv)[0]), _lay_qkv(f(sb_w_qkv)[0])], axis=0)
    wo = np.stack([f(da_w_o)[0], f(sb_w_o)[0]], axis=0)
    cf, bd, cb = _const_tables()
    shared = {
        "gains": f(norm_gains), "win": np.ascontiguousarray(win), "wout": np.ascontiguousarray(wout),
        "wqkv": np.ascontiguousarray(wqkv), "wo": np.ascontiguousarray(wo),
        "wg": f(ple_w_gate), "wp": f(ple_w_proj),
        "lam": f(da_lambda).reshape(256), "subln": f(da_subln).reshape(128),
        "cf": cf, "bd": bd, "cb": cb,
    }
    in_maps = []
    for c in range(N_CORES):
        m = dict(shared)
        m["x"] = np.ascontiguousarray(x[c])
        m["p"] = np.ascontiguousarray(p[:, c])
        in_maps.append(m)
    return in_maps


_NC_CACHE = {}


def kernel(**inputs):
    in_maps = prepare_inputs(**inputs)
    if "nc" not in _NC_CACHE:
        _NC_CACHE["nc"] = build_program()
    nc = _NC_CACHE["nc"]
    res = run_bass_kernel_spmd(nc, in_maps, core_ids=list(range(N_CORES)))
    out = np.stack([np.asarray(r["out"], dtype=np.float32) for r in res.results], axis=0)
    return out.reshape(N_CORES, S_LEN, D)
```

```python
import math
from contextlib import ExitStack

import numpy as np
import concourse.bass as bass
import concourse.mybir as mybir
from concourse.bass_utils import run_bass_kernel_spmd

F32 = mybir.dt.float32
BF16 = mybir.dt.bfloat16
AF = mybir.ActivationFunctionType
ALU = mybir.AluOpType

D = 1024
S_LEN = 2048
NT = 16
FF = 2816
NF = 22
EPS = 1e-6
N_CORES = 8
SAME_ENGINE_SYNC = True

_ESZ = {}


def _esz(dt):
    k = str(dt)
    if k not in _ESZ:
        _ESZ[k] = 2 if ("bfloat16" in k or "float16" in k) else (4 if "32" in k else (1 if "8" in k else 4))
    return _ESZ[k]


class Op:
    __slots__ = ("eng", "fn", "deps", "dsem", "inc", "val", "idx", "isdma")

    def __init__(self, eng, fn, dsem):
        self.eng = eng
        self.fn = fn
        self.deps = set()
        self.dsem = dsem
        self.isdma = dsem is not None
        self.inc = False
        self.val = None


class Sched:
    def __init__(self):
        self.ops = []
        self.bufs = {}
        self.log = None

    @staticmethod
    def region(ap):
        sp = str(ap.space)
        if "SB" not in sp and "PSUM" not in sp:
            return None
        pairs = ap.ap
        off = int(ap.offset)
        esz = _esz(ap.dtype)
        if "PSUM" in sp:
            pstep = pairs[0][0]
            col0 = off % pstep
            ext = 1
            for st, c in pairs[1:]:
                ext += (c - 1) * abs(st)
            lo = (col0 * esz) // 2048 * 2048
            hi = -(-((col0 + ext) * esz) // 2048) * 2048
            return (ap.name, 0, 128, lo, hi)
        pstep, pcount = pairs[0]
        if pstep == 0:
            p0, col0 = 0, off
            pcount = 1
        else:
            p0, col0 = off // pstep, off % pstep
        ext = 1
        for st, c in pairs[1:]:
            ext += (c - 1) * abs(st)
        return (ap.name, p0, p0 + pcount, col0 * esz, (col0 + ext) * esz)

    def add(self, eng, fn, r=(), w=(), dsem=None):
        op = Op(eng, fn, dsem)
        op.idx = len(self.ops)
        self.ops.append(op)
        for ap in r:
            if ap is None or isinstance(ap, (int, float)):
                continue
            rg = self.region(ap)
            if rg is None:
                continue
            name, p0, p1, lo, hi = rg
            b = self.bufs.setdefault(name, {"w": [], "r": []})
            for (q0, q1, l2, h2, o) in b["w"]:
                if q0 < p1 and p0 < q1 and l2 < hi and lo < h2:
                    op.deps.add(o)
            rl = b["r"]
            if name == "psall":
                for (q0, q1, l2, h2, o) in rl:
                    if l2 < hi and lo < h2 and self.ops[o].eng != eng:
                        op.deps.add(o)
            if not op.isdma:
                rl[:] = [x for x in rl if not (x[0] == p0 and x[1] == p1 and x[2] == lo and x[3] == hi
                                                and self.ops[x[4]].eng == eng and not self.ops[x[4]].isdma)]
            rl.append((p0, p1, lo, hi, op.idx))
        for ap in w:
            rg = self.region(ap)
            if rg is None:
                continue
            name, p0, p1, lo, hi = rg
            b = self.bufs.setdefault(name, {"w": [], "r": []})
            for key in ("w", "r"):
                keep = []
                for x in b[key]:
                    (q0, q1, l2, h2, o) = x
                    if q0 < p1 and p0 < q1 and l2 < hi and lo < h2:
                        if o != op.idx:
                            op.deps.add(o)
                        if p0 <= q0 and q1 <= p1 and lo <= l2 and h2 <= hi:
                            continue
                    keep.append(x)
                b[key] = keep
            b["w"].append((p0, p1, lo, hi, op.idx))
        return op

    def finalize(self):
        ops = self.ops
        for op in ops:
            nd = set()
            for j in op.deps:
                d = ops[j]
                if d.isdma:
                    nd.add(j)
                elif d.eng == op.eng and not op.isdma:
                    if op.eng == "pe" or not SAME_ENGINE_SYNC:
                        continue
                    nd.add(j)
                else:
                    nd.add(j)
            latest = {}
            keep = set()
            for j in nd:
                d = ops[j]
                if d.isdma:
                    keep.add(j)
                elif j > latest.get(d.eng, -1):
                    latest[d.eng] = j
            keep.update(latest.values())
            nd = keep
            op.deps = nd
            for j in nd:
                ops[j].inc = True
        cnt = {}
        for op in ops:
            if op.isdma:
                k = ("dma", op.dsem)
                cnt[k] = cnt.get(k, 0) + 16
                op.val = (k, cnt[k])
            elif op.inc:
                k = ("eng", op.eng)
                cnt[k] = cnt.get(k, 0) + 1
                op.val = (k, cnt[k])
        self.final = dict(cnt)
        return sorted(cnt.keys(), key=str)

    def emit(self, nc, sems):
        ops = self.ops
        streams = {}
        for op in ops:
            streams.setdefault(op.eng, []).append(op)
        final = self.final

        def run(engine, name, is_last):
            known = {}
            for op in streams.get(name, []):
                need = {}
                for j in op.deps:
                    k, v = ops[j].val
                    if v > need.get(k, 0):
                        need[k] = v
                for k, v in need.items():
                    if known.get(k, 0) >= v:
                        continue
                    engine.wait_ge(sems[k], v)
                    known[k] = v
                ins = op.fn(engine)
                if self.log is not None:
                    self.log.append((name, op.idx, sorted(need.items(), key=str), op.val, str(ins.concise())[:150]))
                if op.val is not None:
                    k, v = op.val
                    ins.then_inc(sems[k], 16 if op.isdma else 1)
            if is_last:
                for k, v in final.items():
                    if k[0] == "dma":
                        engine.wait_ge(sems[k], v)

        with nc.Block() as block:
            @block.sync
            def _(e):
                run(e, "sp", True)

            @block.gpsimd
            def _(e):
                run(e, "pool", False)

            @block.tensor
            def _(e):
                run(e, "pe", False)

            @block.scalar
            def _(e):
                run(e, "act", False)

            @block.vector
            def _(e):
                run(e, "dve", False)


def _const_tables():
    k = np.arange(128, dtype=np.float64)[:, None]
    q = np.arange(128, dtype=np.float64)[None, :]
    slopes = np.array([2.0 ** (-8.0 * (h + 1) / 8) for h in range(8)])
    bt = np.zeros((128, 8, 16), np.float32)
    for h in range(8):
        for dl in range(16):
            bt[:, h, dl] = slopes[h] * (np.arange(128) - 128.0 * dl - 64.0)
    bd = np.zeros((128, 8, 128), np.float32)
    allowed = (np.floor(k / 64) <= np.floor(q / 64))
    for h in range(8):
        v = -slopes[h] * np.abs(q - k) + slopes[h] * (q - 64.0)
        bd[:, h, :] = np.where(allowed, v, -30000.0)
    m01 = (k < q).astype(np.float32)
    cb = np.zeros((128, 4, 128), np.float32)
    cb[:, 0, :] = np.eye(128)
    cb[:, 1, :] = -(k >= q).astype(np.float32)
    cb[:, 2, :] = -1.0
    cb[:, 3, :] = np.where(k < q, 0.0, -30000.0)
    cf = np.concatenate([bt.reshape(128, 128), m01], axis=1)
    return cf.astype(np.float32), bd.reshape(128, 1024), cb.reshape(128, 512)


def _lay_win(w):
    a = w[:, :FF].reshape(8, 128, NF, 128)
    b = w[:, FF:].reshape(8, 128, NF, 128)
    ab = np.stack([a, b], axis=3)
    return np.ascontiguousarray(ab.transpose(2, 1, 0, 3, 4)).reshape(NF, 128, 8 * 256)


def _lay_qkv(w):
    r = w.reshape(8, 128, 3, 8, 128)
    return np.ascontiguousarray(r.transpose(3, 1, 0, 2, 4)).reshape(8, 128, 8 * 384)


O_WS = 0
O_CB = 6144
O_XN = 6656
O_HT = 8704
O_UT = 16896
O_WOUT = 28160
O_WOA = 41472
O_WOB = 28160
O_SA = 31232
O_JK = 51712
O_OT = 25088
O_QK = 41472
O_V = 53760
O_MISC = 57920
NAB = 61248
F_GN = 0
F_TMP = 2048
F_BD = 3072
F_GE = 4096
F_PF = 4608
F_CF = 5120
F_SG = 5376
F_LAM = 5504
F_ST = 5760
F_JK = 6016
F_PERS = 6144
NAF = 6160


def build_program(stages=None, depth=2, dbg=None):
    dbg = dbg or {}
    if stages is None:
        stages = ["ffn1", "attn", "wo", "ffn2", "ple"]
    nc = bass.Bass("TRN2", target_bir_lowering=False)
    S = Sched()
    dr = {}

    def dram(name, shape, kind="ExternalInput"):
        dr[name] = nc.dram_tensor(name, list(shape), F32, kind=kind).ap()
        return dr[name]

    x_d = dram("x", [S_LEN, D])
    p_d = dram("p", [2, S_LEN, 256])
    g_d = dram("gains", [2, 8, D])
    win_d = dram("win", [4, NF, 128, 2048])
    wout_d = dram("wout", [4, FF, D])
    wqkv_d = dram("wqkv", [2, 8, 128, 3072])
    wo_d = dram("wo", [2, D, D])
    wg_d = dram("wg", [2, D, D])
    wp_d = dram("wp", [2, 256, D])
    lam_d = dram("lam", [256])
    sub_d = dram("subln", [128])
    cf_d = dram("cf", [128, 256])
    bd_d = dram("bd", [128, 1024])
    cb_d = dram("cb", [128, 512])
    out_d = dram("out", [S_LEN, D], kind="ExternalOutput")

    X = nc.alloc_sbuf_tensor("X", [128, NT, D], F32)
    AB = nc.alloc_sbuf_tensor("AB", [128, NAB], BF16)
    AFt = nc.alloc_sbuf_tensor("AFt", [128, NAF], F32)
    PSALL = nc.alloc_psum_tensor("psall", [128, 4096], F32)
    PSD = [PSALL[:, m * 1024:(m + 1) * 1024].rearrange("p (b n) -> p b n", b=2) for m in range(4)]
    PS = [PSALL[:, i * 512:(i + 1) * 512] for i in range(8)]
    PSB = [PS[i].bitcast(BF16) for i in range(8)]

    def ab(o, n):
        return AB[:, o:o + n]

    def af(o, n):
        return AFt[:, o:o + n]

    IDENT = ab(O_CB, 128)
    NEGTRI = ab(O_CB + 128, 128)
    NEGONE = ab(O_CB + 256, 128)
    MDIAG = ab(O_CB + 384, 128)
    HTv = ab(O_HT, 16384).rearrange("p (t c k) -> p t c k", t=16, c=8)
    JKB = ab(O_JK, 512)
    JKF = af(F_JK, 128)
    BT = af(F_CF, 128)
    M01 = af(F_CF + 128, 128)
    SG = af(F_SG, 128)
    LAMB = af(F_LAM, 256)
    NEGHALF = af(F_PERS + 1, 1)

    cnt = {"st": 0, "pt": 0, "ws": 0, "gn": 0}

    def mm(out, lhsT, rhs, start, stop, skip=False):
        def fn(e, out=out, lhsT=lhsT, rhs=rhs, start=start, stop=stop, skip=skip):
            if skip:
                return e.matmul(out, lhsT, rhs, start=start, stop=stop, skip_group_check=True)
            return e.matmul(out, lhsT, rhs, start=start, stop=stop)
        S.add("pe", fn, r=[lhsT, rhs], w=[out])

    def tr(out, in_):
        def fn(e, out=out, in_=in_):
            return e.transpose(out, in_, IDENT)
        S.add("pe", fn, r=[in_, IDENT], w=[out])

    def act(out, in_, func, bias=None, scale=1.0, accum=None, junk=False):
        def fn(e, out=out, in_=in_, func=func, bias=bias, scale=scale, accum=accum):
            kw = {}
            if bias is not None:
                kw["bias"] = bias
            if accum is not None:
                kw["accum_out"] = accum
            return e.activation(out, in_, func, scale=scale, **kw)
        r = [in_]
        if bias is not None and not isinstance(bias, float):
            r.append(bias)
        w = [] if junk else [out]
        if accum is not None:
            w.append(accum)
        S.add("act", fn, r=r, w=w)

    def ts(out, in0, s1, s2, op0, op1=None, eng="dve"):
        def fn(e, out=out, in0=in0, s1=s1, s2=s2, op0=op0, op1=op1):
            if op1 is None:
                return e.tensor_scalar(out, in0, s1, None, op0)
            return e.tensor_scalar(out, in0, s1, s2, op0, op1)
        r = [in0] + [a for a in (s1, s2) if a is not None and not isinstance(a, (int, float))]
        S.add(eng, fn, r=r, w=[out])

    def stt(out, in0, scalar, in1, op0, op1, accum=None, junk=False):
        def fn(e, out=out, in0=in0, scalar=scalar, in1=in1, op0=op0, op1=op1, accum=accum):
            if accum is not None:
                return e.scalar_tensor_tensor(out, in0, scalar, in1, op0, op1, accum_out=accum)
            return e.scalar_tensor_tensor(out, in0, scalar, in1, op0, op1)
        r = [in0, in1] + ([scalar] if not isinstance(scalar, (int, float)) else [])
        w = [] if junk else [out]
        if accum is not None:
            w.append(accum)
        S.add("dve", fn, r=r, w=w)

    def tt(out, in0, in1, op, eng="dve"):
        def fn(e, out=out, in0=in0, in1=in1, op=op):
            return e.tensor_tensor(out, in0, in1, op)
        S.add(eng, fn, r=[in0, in1], w=[out])

    def rstd(rs, ss, epsv):
        t1 = stcol()
        ts(t1, ss, epsv, None, ALU.add, eng="pool")
        tt(rs, t1, NEGHALF, ALU.pow, eng="pool")

    def cp(out, in_, eng="dve"):
        if eng == "act":
            def fn(e, out=out, in_=in_):
                return e.activation(out, in_, AF.Copy)
        else:
            def fn(e, out=out, in_=in_):
                return e.tensor_copy(out, in_)
        S.add(eng, fn, r=[in_], w=[out])

    def recip(out, in_):
        def fn(e, out=out, in_=in_):
            return e.reciprocal(out, in_)
        S.add("dve", fn, r=[in_], w=[out])

    def memset(ap, v):
        def fn(e, ap=ap, v=v):
            return e.memset(ap, v)
        S.add("dve", fn, r=[], w=[ap])

    def dma(q, out, in_, key):
        def fn(e, out=out, in_=in_):
            return e.dma_start(out=out, in_=in_)
        S.add(q, fn, r=[in_], w=[out], dsem=key)

    def stcol():
        c = cnt["st"] % 256
        cnt["st"] += 1
        return af(F_ST + c, 1)

    def ptbank(banks=(6, 7)):
        c = cnt["pt"] % len(banks)
        cnt["pt"] += 1
        return PSB[banks[c]]

    cpeng = {"i": 0}

    def cp_alt(out, in_):
        cpeng["i"] += 1
        cp(out, in_, eng=("act" if cpeng["i"] % 2 else "dve"))

    def load_gain(l, k):
        slot = cnt["gn"] % 2
        cnt["gn"] += 1
        g = af(F_GN + slot * 1024, 1024)
        dma("sp", g, g_d[l, k, :].partition_broadcast(128), ("gn", slot))
        return g

    def prenorm_a(t, gain):
        xn = ab(O_XN + (t % 2) * 1024, 1024)
        ss = stcol()
        rs = stcol()
        act(xn, X[:, t, :], AF.Square, scale=1.0 / 32.0, accum=ss, junk=False)
        rstd(rs, ss, EPS)
        stt(xn, X[:, t, :], rs, gain, ALU.mult, ALU.mult)

    def prenorm_b(t, ht_tile, banks=(6, 7)):
        xn = ab(O_XN + (t % 2) * 1024, 1024)
        pt = ptbank(banks)
        for cc in range(8):
            tr(pt[:, cc * 128:(cc + 1) * 128], xn[:, cc * 128:(cc + 1) * 128])
        cp_alt(ab(O_HT + ht_tile * 1024, 1024), pt)

    def prenorm(t, gain, ht_tile):
        prenorm_a(t, gain)
        prenorm_b(t, ht_tile)

    def prenorm_seq(tiles, gain, slots):
        prenorm_a(tiles[0], gain)
        for idx, t in enumerate(tiles):
            if idx + 1 < len(tiles):
                prenorm_a(tiles[idx + 1], gain)
            prenorm_b(t, slots[idx])

    def postnorm(t, ys, gain, factor, inplace=False):
        sc = (2.0 if factor == 0.5 else 1.0) / 32.0
        epsv = EPS * (4.0 if factor == 0.5 else 1.0)
        s0, s1, s2, rs = stcol(), stcol(), stcol(), stcol()
        tmps = [af(F_TMP + hh * 512, 512) for hh in range(2)]
        if inplace:
            jb = AFt[:, F_GE:F_GE + 512].bitcast(BF16)
            jk = [jb[:, 0:512], jb[:, 512:1024]]
        else:
            jk = tmps
        act(jk[0], ys[0], AF.Square, scale=sc, accum=s0)
        act(jk[1], ys[1], AF.Square, scale=sc, accum=s1)
        tt(s2, s0, s1, ALU.add)
        rstd(rs, s2, epsv)
        for hh in range(2):
            tmp = tmps[hh]
            stt(tmp, ys[hh], rs, gain[:, hh * 512:(hh + 1) * 512], ALU.mult, ALU.mult)
            xs = X[:, t, hh * 512:(hh + 1) * 512]
            tt(xs, xs, tmp, ALU.add)

    prefetched = set()

    def wout_a_dma(wi):
        src = wout_d[wi].rearrange("(f p) d -> p f d", p=128)
        WOAv = ab(O_WOA, 19 * 1024).rearrange("p (f d) -> p f d", f=19)
        dma("pool", WOAv[:, 0:10, :], src[:, 0:10, :], ("wout", 0))
        dma("pool", WOAv[:, 10:19, :], src[:, 10:19, :], ("wout", 1))
        prefetched.add(wi)

    def ffn(l, which):
        wi = l * 2 + which
        gpre = load_gain(l, 0 if which == 0 else 4)
        gpost = load_gain(l, 1 if which == 0 else 5)
        UT = ab(O_UT, NF * 512).rearrange("p (f t) -> p f t", f=NF)
        wout_src = wout_d[wi].rearrange("(f p) d -> p f d", p=128)

        def wout(f):
            if f < 19:
                return ab(O_WOA + f * 1024, 1024)
            return ab(O_WOB + (f - 19) * 1024, 1024)
        prenorm_seq([0, 1, 2, 3], gpre, [0, 1, 2, 3])

        def win_dma(sidx):
            if sidx >= 4 * NF:
                return
            f = sidx % NF
            wt = ab(O_WS + (sidx % 3) * 2048, 2048)
            dma("pool", wt, win_d[wi, f], ("ws", sidx % 3))

        for sidx in range(3):
            win_dma(sidx)
        if wi not in prefetched:
            wout_a_dma(wi)
        WOBv = ab(O_WOB, 3 * 1024).rearrange("p (f d) -> p f d", f=3)
        dma("pool", WOBv, wout_src[:, 19:22, :], ("wout", 2))
        for g in range(4):
            hb = (g % 2) * 4
            for f in range(NF):
                sidx = g * NF + f
                wt = ab(O_WS + (sidx % 3) * 2048, 2048)
                wv = wt.rearrange("p (c n) -> p c n", c=8)
                A = PS[2 * (f % 2)]
                B = PS[2 * (f % 2) + 1]
                Av = A.rearrange("p (t k) -> p t k", t=4)
                Bv = B.rearrange("p (t k) -> p t k", t=4)
                for c in range(8):
                    mm(Av, wv[:, c, 0:128], HTv[:, hb:hb + 4, c, :], c == 0, c == 7)
                for c in range(8):
                    mm(Bv, wv[:, c, 128:256], HTv[:, hb:hb + 4, c, :], c == 0, c == 7)
                win_dma(sidx + 3)
                sa = ab(O_SA + (f % 2) * 512, 512)
                act(sa, A, AF.Silu)
                tt(UT[:, f, :], sa, B, ALU.mult)
            for tt_ in range(4):
                t = 4 * g + tt_
                if g < 3:
                    prenorm_a(4 * (g + 1) + tt_, gpre)
                yb = [(4, 5), (6, 7)][tt_ % 2]
                ys = [PS[yb[0]], PS[yb[1]]]
                for hh in range(2):
                    for f in range(NF):
                        mm(ys[hh], UT[:, f, tt_ * 128:(tt_ + 1) * 128], wout(f)[:, hh * 512:(hh + 1) * 512],
                           f == 0, f == NF - 1)
                postnorm(t, ys, gpost, 0.5)
                if g < 3 and not dbg.get("ffn_nopn"):
                    prenorm_b(4 * (g + 1) + tt_, ((g + 1) % 2) * 4 + tt_, banks=(0, 1))
            if g < 3 and dbg.get("ffn_nopn"):
                for k2 in range(4):
                    prenorm_b(4 * (g + 1) + k2, ((g + 1) % 2) * 4 + k2)

    QKv = ab(O_QK, 12288).rearrange("p (s w n) -> p s w n", s=2, w=3)
    Vv = ab(O_V, 4160).rearrange("p (s t e) -> p s t e", s=2, t=16)
    OTv = ab(O_OT, 16384).rearrange("p (h n) -> p h n", h=8)

    def project(mi, h, bank):
        slot = h % 2
        wt = ab(O_WS + slot * 3072, 3072)
        wv = wt.rearrange("p (c n) -> p c n", c=8)
        dma("pool", wv, wqkv_d[mi, h].rearrange("p (c n) -> p c n", c=8), ("ws", slot))
        QT0 = QKv[:, slot, 0, :]
        QT1 = QKv[:, slot, 1, :]
        KT = QKv[:, slot, 2, :]
        QT = (QT0, QT1)
        P6 = PS[bank]
        P6v = P6.rearrange("p (t k) -> p t k", t=4)
        chunks = []

        def cq(n, half):
            for c in range(4 * half, 4 * half + 4):
                mm(P6v, wv[:, c, 0:128], HTv[:, 4 * n:4 * n + 4, c, :], c == 0, c == 7)
            if half == 1:
                ts(QT0[0:64, n * 512:(n + 1) * 512], P6[0:64, :], 0.125, None, ALU.mult)
                ts(QT1[64:128, n * 512:(n + 1) * 512], P6[64:128, :], 0.125, None, ALU.mult)

        def ck(n, half):
            for c in range(4 * half, 4 * half + 4):
                mm(P6v, wv[:, c, 128:256], HTv[:, 4 * n:4 * n + 4, c, :], c == 0, c == 7)
            if half == 1:
                cp(KT[:, n * 512:(n + 1) * 512], P6, eng="dve")

        def cv(n, k):
            t = 4 * n + k
            for c in range(8):
                mm(P6[:, k * 128:(k + 1) * 128], HTv[:, t, c, :], wv[:, c, 256:384], c == 0, c == 7)
            if k == 3:
                cp(Vv[:, slot, 4 * n:4 * n + 4, 0:128], P6v, eng="dve")

        for n in range(4):
            for half in range(2):
                chunks.append(lambda n=n, half=half: cq(n, half))
            for half in range(2):
                chunks.append(lambda n=n, half=half: ck(n, half))
        for n in range(4):
            for k in range(4):
                chunks.append(lambda n=n, k=k: cv(n, k))
        return QT, KT, Vv[:, slot], chunks

    def zero_q_halves():
        for sl in range(2):
            memset(QKv[64:128, sl, 0, :], 0.0)
            memset(QKv[0:64, sl, 1, :], 0.0)

    def finish_tile(on, h, j):
        pt = PSB[7][:, 0:128]
        tr(pt, on)
        cp(OTv[:, h, j * 128:(j + 1) * 128], pt, eng="dve")

    def attn_da(l):
        lambda_init = 0.8 - 0.6 * math.exp(-0.3 * l)
        g2 = load_gain(l, 2)
        prenorm_seq(list(range(NT)), g2, list(range(NT)))
        l0, l1 = stcol(), stcol()
        stt(JKF[:, 0:64], LAMB[:, 0:64], 1.0, LAMB[:, 64:128], ALU.mult, ALU.mult, accum=l0)
        stt(JKF[:, 64:128], LAMB[:, 128:192], 1.0, LAMB[:, 192:256], ALU.mult, ALU.mult, accum=l1)
        e0, e1, dd = stcol(), stcol(), stcol()
        nlam = af(F_PERS, 1)
        act(e0, l0, AF.Exp)
        act(e1, l1, AF.Exp)
        tt(dd, e1, e0, ALU.subtract)
        ts(nlam, dd, -lambda_init, None, ALU.add)
        ts(SG, SG, 1.0 - lambda_init, None, ALU.mult)
        for s in range(2):
            memset(Vv[:, s, :, 128:129], 1.0)
        zero_q_halves()
        NPT = 8
        PTr = ab(O_MISC, NPT * 256).rearrange("p (s n) -> p s n", s=NPT)
        ONr = ab(O_MISC + NPT * 256, 256).rearrange("p (s n) -> p s n", s=2)
        dc = dbg.get("da_cut", 9)
        if dc < 1:
            return
        LAG = 4
        NH = dbg.get("da_heads", 8)
        projs = {0: project(0, 0, 6)}
        for h in range(NH):
            QT, KT, V, chs = projs[h]
            while chs:
                chs.pop(0)()
            if h == NH - 1:
                load_wo(l)
            nxt = []
            if h + 1 < NH:
                projs[h + 1] = project(0, h + 1, 6)
                nxt = projs[h + 1][3]
            if dc < 2:
                continue
            pairs = [(j, i) for j in range(dbg.get("da_nt", NT)) for i in range(j + 1)]
            n = len(pairs)

            def sbank(k):
                gi_, ix_ = k // 4, k % 4
                return PS[2 * (gi_ % 2) + ix_ // 2][:, (ix_ % 2) * 256:(ix_ % 2) * 256 + 256]

            def score_mm(k, QT=QT, KT=KT):
                j, i = pairs[k]
                Sb = sbank(k)
                for c in range(2):
                    mm(Sb[:, c * 128:(c + 1) * 128], KT[:, i * 128:(i + 1) * 128],
                       QT[c][:, j * 128:(j + 1) * 128], True, True)

            def score_diag(k, h=h):
                j, i = pairs[k]
                if i != j:
                    return
                Sb = sbank(k)
                dt_ = af(F_TMP + (j % 4) * 256, 256)
                for c in range(2):
                    tt(dt_[:, c * 128:(c + 1) * 128], Sb[:, c * 128:(c + 1) * 128],
                       af(F_BD + h * 128, 128), ALU.add)

            def score_act(k, h=h):
                j, i = pairs[k]
                Sb = sbank(k)
                pts = PTr[:, k % NPT, :]
                if i == j:
                    act(pts, af(F_TMP + (j % 4) * 256, 256), AF.Exp)
                else:
                    act(pts, Sb, AF.Exp, bias=af(F_CF + h * 16 + (j - i), 1))

            def av(k, h=h, V=V):
                j, i = pairs[k]
                accb = PS[4 + (j % 2)]
                pts = PTr[:, k % NPT, :]
                for c in range(2):
                    mm(accb[:, c * 129:(c + 1) * 129], pts[:, c * 128:(c + 1) * 128], V[:, i, 0:129],
                       i == 0 and c == 0, i == j, skip=True)
                if i != j:
                    return
                r0, r1, r2, ssq, rs2 = stcol(), stcol(), stcol(), stcol(), stcol()
                recip(r0, accb[:, 128:129])
                recip(r1, accb[:, 257:258])
                tt(r2, r1, nlam, ALU.mult)
                o1 = af(F_GE, 128)
                o2 = af(F_GE + 128 + (j % 3) * 128, 128)
                ts(o1, accb[:, 0:128], r0, None, ALU.mult)
                stt(o2, accb[:, 129:257], r2, o1, ALU.mult, ALU.add)
                stt(ONr[:, j % 2, :], o2, 1.0 / 128.0, o2, ALU.mult, ALU.mult, accum=ssq)
                rstd(rs2, ssq, EPS)

                def tail(o2=o2, rs2=rs2, j=j, h=h):
                    on = ONr[:, j % 2, :]
                    stt(on, o2, rs2, SG, ALU.mult, ALU.mult)
                    finish_tile(on, h, j)
                deferred.append((k + 8, tail))

            deferred = []
            GS = 4
            groups = [list(range(a, min(a + GS, n))) for a in range(0, n, GS)]
            NG = len(groups)
            for k in groups[0]:
                score_mm(k)
            for k in groups[0]:
                score_diag(k)
            for gi in range(NG + 1):
                if gi + 1 < NG:
                    for k in groups[gi + 1]:
                        score_mm(k)
                    for k in groups[gi + 1]:
                        score_diag(k)
                if gi < NG:
                    for k in groups[gi]:
                        score_act(k)
                kk = gi * GS
                while deferred and deferred[0][0] <= kk:
                    deferred.pop(0)[1]()
                if dc >= 4 and gi >= 1:
                    for k in groups[gi - 1]:
                        av(k)
                if nxt:
                    nxt.pop(0)()
            while deferred:
                deferred.pop(0)[1]()

    def attn_sb(l):
        g2 = load_gain(l, 2)
        prenorm_seq(list(range(NT)), g2, list(range(NT)))
        zero_q_halves()
        Lbufs = [ab(O_XN, 1024), ab(O_XN + 1024, 1024), AFt[:, F_GE:F_GE + 512].bitcast(BF16)]
        Lbufs = [x.rearrange("p (h n) -> p h n", h=2) for x in Lbufs]
        ATbufs = [ab(O_MISC + k * 1024, 1024).rearrange("p (h n) -> p h n", h=2) for k in range(3)]
        LCall = AFt[:, F_TMP:F_TMP + 2048].bitcast(BF16)
        LCbufs = [LCall[:, k * 1024:(k + 1) * 1024].rearrange("p (h n) -> p h n", h=2) for k in range(4)]
        Wt = PSD[2]
        ACC = PS[6]
        steps = []
        for hp in range(dbg.get("sb_pairs", 8)):
            for g in range(4):
                for i in range(4 * g + 3, -1, -1):
                    steps.append((hp, g, i))
        ctx = {}
        info = {}

        NP = dbg.get("sb_pairs", 8)

        def ensure_proj(hp):
            if (hp, "proj") not in ctx:
                ctx[(hp, "proj")] = project(1, hp, 7)
            chs = ctx[(hp, "proj")][3]
            while chs:
                chs.pop(0)()
            if hp == NP - 1 and "wo" not in ctx:
                ctx["wo"] = True
                load_wo(l)

        def stage_a(n):
            hp, g, i = steps[n]
            ensure_proj(hp)
            QT, KT, V, _ = ctx[(hp, "proj")]
            r = max(0, i - 4 * g)
            c0 = r * 128
            q0 = g * 512 + c0
            q1 = (g + 1) * 512
            diag = i >= 4 * g
            first = (i == 4 * g + 3)
            Zt = PSD[n % 2]
            for hd in range(2):
                mm(Zt[:, hd, c0:512], KT[:, i * 128:(i + 1) * 128], QT[hd][:, q0:q1], True, True)
            act(Zt[:, :, c0:512], Zt[:, :, c0:512], AF.Exp)
            if diag:
                for hd in range(2):
                    tt(Zt[:, hd, c0:c0 + 128], Zt[:, hd, c0:c0 + 128], M01, ALU.mult)
            info[n] = dict(c0=c0, q0=q0, q1=q1, diag=diag, first=first, QT=QT, KT=KT, V=V, Zt=Zt, i=i)

        def stage_a2(n):
            d = info[n]
            c0, first, i, Zt = d["c0"], d["first"], d["i"], d["Zt"]
            L = Lbufs[n % 3]
            act(L[:, :, c0:512], Zt[:, :, c0:512], AF.Ln, bias=1.0)
            lcprev = None if first else info[n - 1]["lcnew"]
            lcnew = None
            if i > 0:
                lcnew = LCbufs[n % 4]
                if lcprev is None:
                    cp(lcnew[:, :, c0:512], L[:, :, c0:512], eng="dve")
                else:
                    tt(lcnew[:, :, c0:512], lcprev[:, :, c0:512], L[:, :, c0:512], ALU.add)
                if c0 > 0:
                    memset(lcnew[:, :, 0:c0], 0.0)
            d.update(lcprev=lcprev, lcnew=lcnew, L=L)

        def stage_bw(n):
            hp, g, i = steps[n]
            d = info[n]
            c0 = d["c0"]
            for hd in range(2):
                seq = [(d["KT"][:, i * 128:(i + 1) * 128], d["QT"][hd][:, d["q0"]:d["q1"]], Wt[:, hd, c0:512]),
                       (NEGTRI, d["L"][:, hd, c0:512], Wt[:, hd, c0:512])]
                if d["lcprev"] is not None:
                    seq.append((NEGONE, d["lcprev"][:, hd, c0:512], Wt[:, hd, c0:512]))
                if d["diag"]:
                    seq.append((IDENT, MDIAG, Wt[:, hd, c0:c0 + 128]))
                for k, (lt, rh, o) in enumerate(seq):
                    mm(o, lt, rh, k == 0, k == len(seq) - 1, skip=True)

        def stage_bexp(n):
            d = info[n]
            c0 = d["c0"]
            AT = ATbufs[n % 3]
            act(AT[:, :, c0:512], Wt[:, :, c0:512], AF.Exp)

        def stage_bav(n):
            hp, g, i = steps[n]
            d = info[n]
            c0 = d["c0"]
            AT = ATbufs[n % 3]
            for hd in range(2):
                mm(ACC[hd * 64:(hd + 1) * 64, c0:512], d["V"][:, i, hd * 64:(hd + 1) * 64], AT[:, hd, c0:512],
                   d["first"], i == 0, skip=True)
            if i == 0:
                cp(OTv[:, hp, g * 512:(g + 1) * 512], ACC, eng="dve")
            del info[n]["L"]

        N = len(steps)

        def a1(k):
            if 0 <= k < N:
                stage_a(k)

        def a2(k):
            if 0 <= k < N:
                stage_a2(k)

        a1(0)
        a2(0)
        a1(1)
        a2(1)
        loc = 0
        for n in range(N + 1):
            if n < N:
                hp = steps[n][0]
                if n == 0 or steps[n - 1][0] != hp:
                    loc = 0
                    if hp + 1 < NP:
                        ctx[(hp + 1, "proj")] = project(1, hp + 1, 7)
                a1(n + 2)
                stage_bw(n)
                stage_bexp(n)
                a2(n + 2)
            if n >= 1:
                stage_bav(n - 1)
            if n < N:
                if hp + 1 < NP and ctx[(hp + 1, "proj")][3]:
                    ctx[(hp + 1, "proj")][3].pop(0)()
                loc += 1

    WOv = ab(O_HT, 8192).rearrange("p (c d) -> p c d", c=8)

    def load_wo(l):
        src = wo_d[l].rearrange("(c p) d -> p c d", p=128)
        dma("pool", WOv[:, 0:4, :], src[:, 0:4, :], ("wo", 0))
        dma("pool", WOv[:, 4:8, :], src[:, 4:8, :], ("wo", 1))

    def wo_phase(l):
        g3 = load_gain(l, 3)
        if "ffn2" in stages:
            wout_a_dma(l * 2 + 1)
        for t in range(NT):
            yb = [(0, 1), (2, 3)][t % 2]
            ys = [PS[yb[0]][:, :], PS[yb[1]][:, :]]
            for hh in range(2):
                for c in range(8):
                    mm(ys[hh], OTv[:, c, t * 128:(t + 1) * 128], WOv[:, c, hh * 512:(hh + 1) * 512], c == 0, c == 7)
            postnorm(t, ys, g3, 1.0)

    def ple(l, store):
        WG = ab(O_OT, 8192).rearrange("p (c d) -> p c d", c=8)
        WP = ab(O_OT + 8192, 2048).rearrange("p (c d) -> p c d", c=2)
        sg = wg_d[l].rearrange("(c p) d -> p c d", p=128)
        dma("pool", WG[:, 0:4, :], sg[:, 0:4, :], ("wg", 0))
        dma("pool", WG[:, 4:8, :], sg[:, 4:8, :], ("wg", 1))
        dma("pool", WP, wp_d[l].rearrange("(c p) d -> p c d", p=128), ("wp", 0))
        g6 = load_gain(l, 6)
        g7 = load_gain(l, 7)
        if l + 1 < depth and "ffn1" in stages:
            wout_a_dma((l + 1) * 2)
        PBr = ab(O_OT + 10240, 1024).rearrange("p (s n) -> p s n", s=4)
        PTTr = ab(O_OT + 11264, 512).rearrange("p (s n) -> p s n", s=2)

        def banks(t):
            return ((0, 1) if t % 2 == 0 else (4, 5)), (2, 3)

        def pload(t):
            if t < NT:
                dma("pool", PBr[:, t % 4, :], p_d[l, t * 128:(t + 1) * 128, :], ("pf", t % 4))

        def front(t):
            pb = PBr[:, t % 4, :]
            pt = ptbank((6, 7))
            for c2 in range(2):
                tr(pt[:, c2 * 128:(c2 + 1) * 128], pb[:, c2 * 128:(c2 + 1) * 128])
            cp(PTTr[:, t % 2, :], pt[:, 0:256], eng="dve")

        def matmuls(t, hslot):
            Gb, Eb = banks(t)
            ptt = PTTr[:, t % 2, :]
            for hh in range(2):
                Gt = PS[Gb[hh]]
                Et = PS[Eb[hh]]
                for c in range(8):
                    mm(Gt, HTv[:, hslot, c, :], WG[:, c, hh * 512:(hh + 1) * 512], c == 0, c == 7)
                for c2 in range(2):
                    mm(Et, ptt[:, c2 * 128:(c2 + 1) * 128], WP[:, c2, hh * 512:(hh + 1) * 512], c2 == 0, c2 == 1)

        def elementwise(t):
            Gb, Eb = banks(t)
            ges = []
            for hh in range(2):
                gs = af(F_BD + hh * 512, 512)
                act(gs, PS[Gb[hh]], AF.Sigmoid)
                ge = af(F_TMP + hh * 512, 512)
                tt(ge, gs, PS[Eb[hh]], ALU.mult)
                ges.append(ge)
            postnorm(t, ges, g7, 1.0, inplace=True)
            if store:
                dma("sp", out_d[t * 128:(t + 1) * 128, :], X[:, t, :], ("x", t))

        for t in range(3):
            pload(t)
        prenorm_seq([0, 1, 2, 3], g6, [0, 1, 2, 3])
        front(0)
        for t in range(NT):
            g, k = t // 4, t % 4
            pload(t + 3)
            if g < 3:
                prenorm_a(4 * (g + 1) + k, g6)
            matmuls(t, (g % 2) * 4 + k)
            if g < 3:
                prenorm_b(4 * (g + 1) + k, ((g + 1) % 2) * 4 + k)
            if t + 1 < NT:
                front(t + 1)
            elementwise(t)

    for t in range(NT):
        dma("sp", X[:, t, :], x_d[t * 128:(t + 1) * 128, :], ("x", t))
    S.add("pool", lambda e: e.memset(NEGHALF, -0.5), r=[], w=[NEGHALF])
    dma("pool", ab(O_CB, 512), cb_d, ("cst", 0))
    dma("sp", af(F_CF, 256), cf_d, ("cst", 1))
    dma("sp", af(F_BD, 1024), bd_d, ("cst", 2))
    dma("sp", LAMB, lam_d.partition_broadcast(128), ("cst", 3))
    dma("sp", SG, sub_d.partition_broadcast(128), ("cst", 4))

    if stages is None:
        stages = ["ffn1", "attn", "wo", "ffn2", "ple"]
    last_stage_store = False
    for l in range(depth):
        final_layer = (l == depth - 1)
        if "ffn1" in stages:
            ffn(l, 0)
        if "attn" in stages:
            if l == 0:
                attn_da(l)
            else:
                attn_sb(l)
            if "wo" in stages:
                wo_phase(l)
        if "ffn2" in stages:
            ffn(l, 1)
        if "ple" in stages:
            ple(l, store=final_layer)
            last_stage_store = final_layer
    if not last_stage_store:
        for t in range(NT):
            dma("sp", out_d[t * 128:(t + 1) * 128, :], X[:, t, :], ("x", t))

    keys = S.finalize()
    if dbg.get('log'):
        S.log = []
        nc._sched = S
    with ExitStack() as es:
        sems = {}
        for i, k in enumerate(keys):
            sems[k] = es.enter_context(nc.semaphore("s%d" % i))
        S.emit(nc, sems)
    return nc


def prepare_inputs(x, p, norm_gains, ffn1_w_in, ffn1_w_out, ffn2_w_in, ffn2_w_out,
                   da_w_qkv, da_w_o, da_lambda, da_subln, sb_w_qkv, sb_w_o, ple_w_proj, ple_w_gate):
    f = lambda a: np.ascontiguousarray(np.asarray(a, dtype=np.float32))
    x, p = f(x), f(p)
    win = np.stack([_lay_win(f(ffn1_w_in)[0]), _lay_win(f(ffn2_w_in)[0]),
                    _lay_win(f(ffn1_w_in)[1]), _lay_win(f(ffn2_w_in)[1])], axis=0)
    wout = np.stack([f(ffn1_w_out)[0], f(ffn2_w_out)[0], f(ffn1_w_out)[1], f(ffn2_w_out)[1]], axis=0)
    wqkv = np.stack([_lay_qkv(f(da_w_qkv)[0]), _lay_qkv(f(sb_w_qkv)[0])], axis=0)
    wo = np.stack([f(da_w_o)[0], f(sb_w_o)[0]], axis=0)
    cf, bd, cb = _const_tables()
    shared = {
        "gains": f(norm_gains), "win": np.ascontiguousarray(win), "wout": np.ascontiguousarray(wout),
        "wqkv": np.ascontiguousarray(wqkv), "wo": np.ascontiguousarray(wo),
        "wg": f(ple_w_gate), "wp": f(ple_w_proj),
        "lam": f(da_lambda).reshape(256), "subln": f(da_subln).reshape(128),
        "cf": cf, "bd": bd, "cb": cb,
    }
    in_maps = []
    for c in range(N_CORES):
        m = dict(shared)
        m["x"] = np.ascontiguousarray(x[c])
        m["p"] = np.ascontiguousarray(p[:, c])
        in_maps.append(m)
    return in_maps


_NC_CACHE = {}


def kernel(**inputs):
    in_maps = prepare_inputs(**inputs)
    if "nc" not in _NC_CACHE:
        _NC_CACHE["nc"] = build_program()
    nc = _NC_CACHE["nc"]
    res = run_bass_kernel_spmd(nc, in_maps, core_ids=list(range(N_CORES)))
    out = np.stack([np.asarray(r["out"], dtype=np.float32) for r in res.results], axis=0)
    return out.reshape(N_CORES, S_LEN, D)
```

```python
import math
from contextlib import ExitStack

import numpy as np
import concourse.bass as bass
import concourse.mybir as mybir
from concourse.bass_utils import run_bass_kernel_spmd

F32 = mybir.dt.float32
BF16 = mybir.dt.bfloat16
AF = mybir.ActivationFunctionType
ALU = mybir.AluOpType

D = 1024
S_LEN = 2048
NT = 16
FF = 2816
NF = 22
EPS = 1e-6
N_CORES = 8
SAME_ENGINE_SYNC = True

_ESZ = {}


def _esz(dt):
    k = str(dt)
    if k not in _ESZ:
        _ESZ[k] = 2 if ("bfloat16" in k or "float16" in k) else (4 if "32" in k else (1 if "8" in k else 4))
    return _ESZ[k]


class Op:
    __slots__ = ("eng", "fn", "deps", "dsem", "inc", "val", "idx", "isdma")

    def __init__(self, eng, fn, dsem):
        self.eng = eng
        self.fn = fn
        self.deps = set()
        self.dsem = dsem
        self.isdma = dsem is not None
        self.inc = False
        self.val = None


class Sched:
    def __init__(self):
        self.ops = []
        self.bufs = {}
        self.log = None

    @staticmethod
    def region(ap):
        sp = str(ap.space)
        if "SB" not in sp and "PSUM" not in sp:
            return None
        pairs = ap.ap
        off = int(ap.offset)
        esz = _esz(ap.dtype)
        if "PSUM" in sp:
            pstep = pairs[0][0]
            col0 = off % pstep
            ext = 1
            for st, c in pairs[1:]:
                ext += (c - 1) * abs(st)
            lo = (col0 * esz) // 2048 * 2048
            hi = -(-((col0 + ext) * esz) // 2048) * 2048
            return (ap.name, 0, 128, lo, hi)
        pstep, pcount = pairs[0]
        if pstep == 0:
            p0, col0 = 0, off
            pcount = 1
        else:
            p0, col0 = off // pstep, off % pstep
        ext = 1
        for st, c in pairs[1:]:
            ext += (c - 1) * abs(st)
        return (ap.name, p0, p0 + pcount, col0 * esz, (col0 + ext) * esz)

    def add(self, eng, fn, r=(), w=(), dsem=None):
        op = Op(eng, fn, dsem)
        op.idx = len(self.ops)
        self.ops.append(op)
        for ap in r:
            if ap is None or isinstance(ap, (int, float)):
                continue
            rg = self.region(ap)
            if rg is None:
                continue
            name, p0, p1, lo, hi = rg
            b = self.bufs.setdefault(name, {"w": [], "r": []})
            for (q0, q1, l2, h2, o) in b["w"]:
                if q0 < p1 and p0 < q1 and l2 < hi and lo < h2:
                    op.deps.add(o)
            rl = b["r"]
            if name == "psall":
                for (q0, q1, l2, h2, o) in rl:
                    if l2 < hi and lo < h2 and self.ops[o].eng != eng:
                        op.deps.add(o)
            if not op.isdma:
                rl[:] = [x for x in rl if not (x[0] == p0 and x[1] == p1 and x[2] == lo and x[3] == hi
                                                and self.ops[x[4]].eng == eng and not self.ops[x[4]].isdma)]
            rl.append((p0, p1, lo, hi, op.idx))
        for ap in w:
            rg = self.region(ap)
            if rg is None:
                continue
            name, p0, p1, lo, hi = rg
            b = self.bufs.setdefault(name, {"w": [], "r": []})
            for key in ("w", "r"):
                keep = []
                for x in b[key]:
                    (q0, q1, l2, h2, o) = x
                    if q0 < p1 and p0 < q1 and l2 < hi and lo < h2:
                        if o != op.idx:
                            op.deps.add(o)
                        if p0 <= q0 and q1 <= p1 and lo <= l2 and h2 <= hi:
                            continue
                    keep.append(x)
                b[key] = keep
            b["w"].append((p0, p1, lo, hi, op.idx))
        return op

    def finalize(self):
        ops = self.ops
        for op in ops:
            nd = set()
            for j in op.deps:
                d = ops[j]
                if d.isdma:
                    nd.add(j)
                elif d.eng == op.eng and not op.isdma:
                    if op.eng == "pe" or not SAME_ENGINE_SYNC:
                        continue
                    nd.add(j)
                else:
                    nd.add(j)
            latest = {}
            keep = set()
            for j in nd:
                d = ops[j]
                if d.isdma:
                    keep.add(j)
                elif j > latest.get(d.eng, -1):
                    latest[d.eng] = j
            keep.update(latest.values())
            nd = keep
            op.deps = nd
            for j in nd:
                ops[j].inc = True
        cnt = {}
        for op in ops:
            if op.isdma:
                k = ("dma", op.dsem)
                cnt[k] = cnt.get(k, 0) + 16
                op.val = (k, cnt[k])
            elif op.inc:
                k = ("eng", op.eng)
                cnt[k] = cnt.get(k, 0) + 1
                op.val = (k, cnt[k])
        self.final = dict(cnt)
        return sorted(cnt.keys(), key=str)

    def emit(self, nc, sems):
        ops = self.ops
        streams = {}
        for op in ops:
            streams.setdefault(op.eng, []).append(op)
        final = self.final

        def run(engine, name, is_last):
            known = {}
            for op in streams.get(name, []):
                need = {}
                for j in op.deps:
                    k, v = ops[j].val
                    if v > need.get(k, 0):
                        need[k] = v
                for k, v in need.items():
                    if known.get(k, 0) >= v:
                        continue
                    engine.wait_ge(sems[k], v)
                    known[k] = v
                ins = op.fn(engine)
                if self.log is not None:
                    self.log.append((name, op.idx, sorted(need.items(), key=str), op.val, str(ins.concise())[:150]))
                if op.val is not None:
                    k, v = op.val
                    ins.then_inc(sems[k], 16 if op.isdma else 1)
            if is_last:
                for k, v in final.items():
                    if k[0] == "dma":
                        engine.wait_ge(sems[k], v)

        with nc.Block() as block:
            @block.sync
            def _(e):
                run(e, "sp", True)

            @block.gpsimd
            def _(e):
                run(e, "pool", False)

            @block.tensor
            def _(e):
                run(e, "pe", False)

            @block.scalar
            def _(e):
                run(e, "act", False)

            @block.vector
            def _(e):
                run(e, "dve", False)


def _const_tables():
    k = np.arange(128, dtype=np.float64)[:, None]
    q = np.arange(128, dtype=np.float64)[None, :]
    slopes = np.array([2.0 ** (-8.0 * (h + 1) / 8) for h in range(8)])
    bt = np.zeros((128, 8, 16), np.float32)
    for h in range(8):
        for dl in range(16):
            bt[:, h, dl] = slopes[h] * (np.arange(128) - 128.0 * dl - 64.0)
    bd = np.zeros((128, 8, 128), np.float32)
    allowed = (np.floor(k / 64) <= np.floor(q / 64))
    for h in range(8):
        v = -slopes[h] * np.abs(q - k) + slopes[h] * (q - 64.0)
        bd[:, h, :] = np.where(allowed, v, -30000.0)
    m01 = (k < q).astype(np.float32)
    cb = np.zeros((128, 4, 128), np.float32)
    cb[:, 0, :] = np.eye(128)
    cb[:, 1, :] = -(k >= q).astype(np.float32)
    cb[:, 2, :] = -1.0
    cb[:, 3, :] = np.where(k < q, 0.0, -30000.0)
    cf = np.concatenate([bt.reshape(128, 128), m01], axis=1)
    return cf.astype(np.float32), bd.reshape(128, 1024), cb.reshape(128, 512)


def _lay_win(w):
    a = w[:, :FF].reshape(8, 128, NF, 128)
    b = w[:, FF:].reshape(8, 128, NF, 128)
    ab = np.stack([a, b], axis=3)
    return np.ascontiguousarray(ab.transpose(2, 1, 0, 3, 4)).reshape(NF, 128, 8 * 256)


def _lay_qkv(w):
    r = w.reshape(8, 128, 3, 8, 128)
    return np.ascontiguousarray(r.transpose(3, 1, 0, 2, 4)).reshape(8, 128, 8 * 384)


O_WS = 0
O_CB = 6144
O_XN = 6656
O_HT = 8704
O_UT = 16896
O_WOUT = 28160
O_WOA = 41472
O_WOB = 28160
O_SA = 31232
O_JK = 51712
O_OT = 25088
O_QK = 41472
O_V = 53760
O_MISC = 57920
NAB = 61248
F_GN = 0
F_TMP = 2048
F_BD = 3072
F_GE = 4096
F_PF = 4608
F_CF = 5120
F_SG = 5376
F_LAM = 5504
F_ST = 5760
F_JK = 6016
F_PERS = 6144
NAF = 6160


def build_program(stages=None, depth=2, dbg=None):
    dbg = dbg or {}
    if stages is None:
        stages = ["ffn1", "attn", "wo", "ffn2", "ple"]
    nc = bass.Bass("TRN2", target_bir_lowering=False)
    S = Sched()
    dr = {}

    def dram(name, shape, kind="ExternalInput"):
        dr[name] = nc.dram_tensor(name, list(shape), F32, kind=kind).ap()
        return dr[name]

    x_d = dram("x", [S_LEN, D])
    p_d = dram("p", [2, S_LEN, 256])
    g_d = dram("gains", [2, 8, D])
    win_d = dram("win", [4, NF, 128, 2048])
    wout_d = dram("wout", [4, FF, D])
    wqkv_d = dram("wqkv", [2, 8, 128, 3072])
    wo_d = dram("wo", [2, D, D])
    wg_d = dram("wg", [2, D, D])
    wp_d = dram("wp", [2, 256, D])
    lam_d = dram("lam", [256])
    sub_d = dram("subln", [128])
    cf_d = dram("cf", [128, 256])
    bd_d = dram("bd", [128, 1024])
    cb_d = dram("cb", [128, 512])
    out_d = dram("out", [S_LEN, D], kind="ExternalOutput")

    X = nc.alloc_sbuf_tensor("X", [128, NT, D], F32)
    AB = nc.alloc_sbuf_tensor("AB", [128, NAB], BF16)
    AFt = nc.alloc_sbuf_tensor("AFt", [128, NAF], F32)
    PSALL = nc.alloc_psum_tensor("psall", [128, 4096], F32)
    PSD = [PSALL[:, m * 1024:(m + 1) * 1024].rearrange("p (b n) -> p b n", b=2) for m in range(4)]
    PS = [PSALL[:, i * 512:(i + 1) * 512] for i in range(8)]
    PSB = [PS[i].bitcast(BF16) for i in range(8)]

    def ab(o, n):
        return AB[:, o:o + n]

    def af(o, n):
        return AFt[:, o:o + n]

    IDENT = ab(O_CB, 128)
    NEGTRI = ab(O_CB + 128, 128)
    NEGONE = ab(O_CB + 256, 128)
    MDIAG = ab(O_CB + 384, 128)
    HTv = ab(O_HT, 16384).rearrange("p (t c k) -> p t c k", t=16, c=8)
    JKB = ab(O_JK, 512)
    JKF = af(F_JK, 128)
    BT = af(F_CF, 128)
    M01 = af(F_CF + 128, 128)
    SG = af(F_SG, 128)
    LAMB = af(F_LAM, 256)
    NEGHALF = af(F_PERS + 1, 1)

    cnt = {"st": 0, "pt": 0, "ws": 0, "gn": 0}

    def mm(out, lhsT, rhs, start, stop, skip=False):
        def fn(e, out=out, lhsT=lhsT, rhs=rhs, start=start, stop=stop, skip=skip):
            if skip:
                return e.matmul(out, lhsT, rhs, start=start, stop=stop, skip_group_check=True)
            return e.matmul(out, lhsT, rhs, start=start, stop=stop)
        S.add("pe", fn, r=[lhsT, rhs], w=[out])

    def tr(out, in_):
        def fn(e, out=out, in_=in_):
            return e.transpose(out, in_, IDENT)
        S.add("pe", fn, r=[in_, IDENT], w=[out])

    def act(out, in_, func, bias=None, scale=1.0, accum=None, junk=False):
        def fn(e, out=out, in_=in_, func=func, bias=bias, scale=scale, accum=accum):
            kw = {}
            if bias is not None:
                kw["bias"] = bias
            if accum is not None:
                kw["accum_out"] = accum
            return e.activation(out, in_, func, scale=scale, **kw)
        r = [in_]
        if bias is not None and not isinstance(bias, float):
            r.append(bias)
        w = [] if junk else [out]
        if accum is not None:
            w.append(accum)
        S.add("act", fn, r=r, w=w)

    def ts(out, in0, s1, s2, op0, op1=None, eng="dve"):
        def fn(e, out=out, in0=in0, s1=s1, s2=s2, op0=op0, op1=op1):
            if op1 is None:
                return e.tensor_scalar(out, in0, s1, None, op0)
            return e.tensor_scalar(out, in0, s1, s2, op0, op1)
        r = [in0] + [a for a in (s1, s2) if a is not None and not isinstance(a, (int, float))]
        S.add(eng, fn, r=r, w=[out])

    def stt(out, in0, scalar, in1, op0, op1, accum=None, junk=False):
        def fn(e, out=out, in0=in0, scalar=scalar, in1=in1, op0=op0, op1=op1, accum=accum):
            if accum is not None:
                return e.scalar_tensor_tensor(out, in0, scalar, in1, op0, op1, accum_out=accum)
            return e.scalar_tensor_tensor(out, in0, scalar, in1, op0, op1)
        r = [in0, in1] + ([scalar] if not isinstance(scalar, (int, float)) else [])
        w = [] if junk else [out]
        if accum is not None:
            w.append(accum)
        S.add("dve", fn, r=r, w=w)

    def tt(out, in0, in1, op, eng="dve"):
        def fn(e, out=out, in0=in0, in1=in1, op=op):
            return e.tensor_tensor(out, in0, in1, op)
        S.add(eng, fn, r=[in0, in1], w=[out])

    def rstd(rs, ss, epsv):
        t1 = stcol()
        ts(t1, ss, epsv, None, ALU.add, eng="pool")
        tt(rs, t1, NEGHALF, ALU.pow, eng="pool")

    def cp(out, in_, eng="dve"):
        if eng == "act":
            def fn(e, out=out, in_=in_):
                return e.activation(out, in_, AF.Copy)
        else:
            def fn(e, out=out, in_=in_):
                return e.tensor_copy(out, in_)
        S.add(eng, fn, r=[in_], w=[out])

    def recip(out, in_):
        def fn(e, out=out, in_=in_):
            return e.reciprocal(out, in_)
        S.add("dve", fn, r=[in_], w=[out])

    def memset(ap, v):
        def fn(e, ap=ap, v=v):
            return e.memset(ap, v)
        S.add("dve", fn, r=[], w=[ap])

    def dma(q, out, in_, key):
        def fn(e, out=out, in_=in_):
            return e.dma_start(out=out, in_=in_)
        S.add(q, fn, r=[in_], w=[out], dsem=key)

    def stcol():
        c = cnt["st"] % 256
        cnt["st"] += 1
        return af(F_ST + c, 1)

    def ptbank(banks=(6, 7)):
        c = cnt["pt"] % len(banks)
        cnt["pt"] += 1
        return PSB[banks[c]]

    cpeng = {"i": 0}

    def cp_alt(out, in_):
        cpeng["i"] += 1
        cp(out, in_, eng=("act" if cpeng["i"] % 2 else "dve"))

    def load_gain(l, k):
        slot = cnt["gn"] % 2
        cnt["gn"] += 1
        g = af(F_GN + slot * 1024, 1024)
        dma("sp", g, g_d[l, k, :].partition_broadcast(128), ("gn", slot))
        return g

    def prenorm_a(t, gain):
        xn = ab(O_XN + (t % 2) * 1024, 1024)
        ss = stcol()
        rs = stcol()
        act(xn, X[:, t, :], AF.Square, scale=1.0 / 32.0, accum=ss, junk=False)
        rstd(rs, ss, EPS)
        stt(xn, X[:, t, :], rs, gain, ALU.mult, ALU.mult)

    def prenorm_b(t, ht_tile, banks=(6, 7)):
        xn = ab(O_XN + (t % 2) * 1024, 1024)
        pt = ptbank(banks)
        for cc in range(8):
            tr(pt[:, cc * 128:(cc + 1) * 128], xn[:, cc * 128:(cc + 1) * 128])
        cp_alt(ab(O_HT + ht_tile * 1024, 1024), pt)

    def prenorm(t, gain, ht_tile):
        prenorm_a(t, gain)
        prenorm_b(t, ht_tile)

    def prenorm_seq(tiles, gain, slots):
        prenorm_a(tiles[0], gain)
        for idx, t in enumerate(tiles):
            if idx + 1 < len(tiles):
                prenorm_a(tiles[idx + 1], gain)
            prenorm_b(t, slots[idx])

    def postnorm(t, ys, gain, factor, inplace=False):
        sc = (2.0 if factor == 0.5 else 1.0) / 32.0
        epsv = EPS * (4.0 if factor == 0.5 else 1.0)
        s0, s1, s2, rs = stcol(), stcol(), stcol(), stcol()
        tmps = [af(F_TMP + hh * 512, 512) for hh in range(2)]
        if inplace:
            jb = AFt[:, F_GE:F_GE + 512].bitcast(BF16)
            jk = [jb[:, 0:512], jb[:, 512:1024]]
        else:
            jk = tmps
        act(jk[0], ys[0], AF.Square, scale=sc, accum=s0)
        act(jk[1], ys[1], AF.Square, scale=sc, accum=s1)
        tt(s2, s0, s1, ALU.add)
        rstd(rs, s2, epsv)
        for hh in range(2):
            tmp = tmps[hh]
            stt(tmp, ys[hh], rs, gain[:, hh * 512:(hh + 1) * 512], ALU.mult, ALU.mult)
            xs = X[:, t, hh * 512:(hh + 1) * 512]
            tt(xs, xs, tmp, ALU.add)

    prefetched = set()

    def wout_a_dma(wi):
        src = wout_d[wi].rearrange("(f p) d -> p f d", p=128)
        WOAv = ab(O_WOA, 19 * 1024).rearrange("p (f d) -> p f d", f=19)
        dma("pool", WOAv[:, 0:10, :], src[:, 0:10, :], ("wout", 0))
        dma("pool", WOAv[:, 10:19, :], src[:, 10:19, :], ("wout", 1))
        prefetched.add(wi)

    def ffn(l, which):
        wi = l * 2 + which
        gpre = load_gain(l, 0 if which == 0 else 4)
        gpost = load_gain(l, 1 if which == 0 else 5)
        UT = ab(O_UT, NF * 512).rearrange("p (f t) -> p f t", f=NF)
        wout_src = wout_d[wi].rearrange("(f p) d -> p f d", p=128)

        def wout(f):
            if f < 19:
                return ab(O_WOA + f * 1024, 1024)
            return ab(O_WOB + (f - 19) * 1024, 1024)
        prenorm_seq([0, 1, 2, 3], gpre, [0, 1, 2, 3])

        def win_dma(sidx):
            if sidx >= 4 * NF:
                return
            f = sidx % NF
            wt = ab(O_WS + (sidx % 3) * 2048, 2048)
            dma("pool", wt, win_d[wi, f], ("ws", sidx % 3))

        for sidx in range(3):
            win_dma(sidx)
        if wi not in prefetched:
            wout_a_dma(wi)
        WOBv = ab(O_WOB, 3 * 1024).rearrange("p (f d) -> p f d", f=3)
        dma("pool", WOBv, wout_src[:, 19:22, :], ("wout", 2))
        for g in range(4):
            hb = (g % 2) * 4
            for f in range(NF):
                sidx = g * NF + f
                wt = ab(O_WS + (sidx % 3) * 2048, 2048)
                wv = wt.rearrange("p (c n) -> p c n", c=8)
                A = PS[2 * (f % 2)]
                B = PS[2 * (f % 2) + 1]
                Av = A.rearrange("p (t k) -> p t k", t=4)
                Bv = B.rearrange("p (t k) -> p t k", t=4)
                for c in range(8):
                    mm(Av, wv[:, c, 0:128], HTv[:, hb:hb + 4, c, :], c == 0, c == 7)
                for c in range(8):
                    mm(Bv, wv[:, c, 128:256], HTv[:, hb:hb + 4, c, :], c == 0, c == 7)
                win_dma(sidx + 3)
                sa = ab(O_SA + (f % 2) * 512, 512)
                act(sa, A, AF.Silu)
                tt(UT[:, f, :], sa, B, ALU.mult)
            for tt_ in range(4):
                t = 4 * g + tt_
                if g < 3:
                    prenorm_a(4 * (g + 1) + tt_, gpre)
                yb = [(4, 5), (6, 7)][tt_ % 2]
                ys = [PS[yb[0]], PS[yb[1]]]
                for hh in range(2):
                    for f in range(NF):
                        mm(ys[hh], UT[:, f, tt_ * 128:(tt_ + 1) * 128], wout(f)[:, hh * 512:(hh + 1) * 512],
                           f == 0, f == NF - 1)
                postnorm(t, ys, gpost, 0.5)
                if g < 3 and not dbg.get("ffn_nopn"):
                    prenorm_b(4 * (g + 1) + tt_, ((g + 1) % 2) * 4 + tt_, banks=(0, 1))
            if g < 3 and dbg.get("ffn_nopn"):
                for k2 in range(4):
                    prenorm_b(4 * (g + 1) + k2, ((g + 1) % 2) * 4 + k2)

    QKv = ab(O_QK, 12288).rearrange("p (s w n) -> p s w n", s=2, w=3)
    Vv = ab(O_V, 4160).rearrange("p (s t e) -> p s t e", s=2, t=16)
    OTv = ab(O_OT, 16384).rearrange("p (h n) -> p h n", h=8)

    def project(mi, h, bank):
        slot = h % 2
        wt = ab(O_WS + slot * 3072, 3072)
        wv = wt.rearrange("p (c n) -> p c n", c=8)
        dma("pool", wv, wqkv_d[mi, h].rearrange("p (c n) -> p c n", c=8), ("ws", slot))
        QT0 = QKv[:, slot, 0, :]
        QT1 = QKv[:, slot, 1, :]
        KT = QKv[:, slot, 2, :]
        QT = (QT0, QT1)
        P6 = PS[bank]
        P6v = P6.rearrange("p (t k) -> p t k", t=4)
        chunks = []

        def cq(n, half):
            for c in range(4 * half, 4 * half + 4):
                mm(P6v, wv[:, c, 0:128], HTv[:, 4 * n:4 * n + 4, c, :], c == 0, c == 7)
            if half == 1:
                ts(QT0[0:64, n * 512:(n + 1) * 512], P6[0:64, :], 0.125, None, ALU.mult)
                ts(QT1[64:128, n * 512:(n + 1) * 512], P6[64:128, :], 0.125, None, ALU.mult)

        def ck(n, half):
            for c in range(4 * half, 4 * half + 4):
                mm(P6v, wv[:, c, 128:256], HTv[:, 4 * n:4 * n + 4, c, :], c == 0, c == 7)
            if half == 1:
                cp(KT[:, n * 512:(n + 1) * 512], P6, eng="dve")

        def cv(n, k):
            t = 4 * n + k
            for c in range(8):
                mm(P6[:, k * 128:(k + 1) * 128], HTv[:, t, c, :], wv[:, c, 256:384], c == 0, c == 7)
            if k == 3:
                cp(Vv[:, slot, 4 * n:4 * n + 4, 0:128], P6v, eng="dve")

        for n in range(4):
            for half in range(2):
                chunks.append(lambda n=n, half=half: cq(n, half))
            for half in range(2):
                chunks.append(lambda n=n, half=half: ck(n, half))
        for n in range(4):
            for k in range(4):
                chunks.append(lambda n=n, k=k: cv(n, k))
        return QT, KT, Vv[:, slot], chunks

    def zero_q_halves():
        for sl in range(2):
            memset(QKv[64:128, sl, 0, :], 0.0)
            memset(QKv[0:64, sl, 1, :], 0.0)

    def finish_tile(on, h, j):
        pt = PSB[7][:, 0:128]
        tr(pt, on)
        cp_alt(OTv[:, h, j * 128:(j + 1) * 128], pt)

    def attn_da(l):
        lambda_init = 0.8 - 0.6 * math.exp(-0.3 * l)
        g2 = load_gain(l, 2)
        prenorm_seq(list(range(NT)), g2, list(range(NT)))
        l0, l1 = stcol(), stcol()
        stt(JKF[:, 0:64], LAMB[:, 0:64], 1.0, LAMB[:, 64:128], ALU.mult, ALU.mult, accum=l0)
        stt(JKF[:, 64:128], LAMB[:, 128:192], 1.0, LAMB[:, 192:256], ALU.mult, ALU.mult, accum=l1)
        e0, e1, dd = stcol(), stcol(), stcol()
        nlam = af(F_PERS, 1)
        act(e0, l0, AF.Exp)
        act(e1, l1, AF.Exp)
        tt(dd, e1, e0, ALU.subtract)
        ts(nlam, dd, -lambda_init, None, ALU.add)
        ts(SG, SG, 1.0 - lambda_init, None, ALU.mult)
        for s in range(2):
            memset(Vv[:, s, :, 128:129], 1.0)
        zero_q_halves()
        NPT = 8
        PTr = ab(O_MISC, NPT * 256).rearrange("p (s n) -> p s n", s=NPT)
        ONr = ab(O_MISC + NPT * 256, 256).rearrange("p (s n) -> p s n", s=2)
        dc = dbg.get("da_cut", 9)
        if dc < 1:
            return
        LAG = 4
        NH = dbg.get("da_heads", 8)
        projs = {0: project(0, 0, 6)}
        for h in range(NH):
            QT, KT, V, chs = projs[h]
            while chs:
                chs.pop(0)()
            if h == NH - 1:
                load_wo(l)
            nxt = []
            if h + 1 < NH:
                projs[h + 1] = project(0, h + 1, 6)
                nxt = projs[h + 1][3]
            if dc < 2:
                continue
            pairs = [(j, i) for j in range(dbg.get("da_nt", NT)) for i in range(j + 1)]
            n = len(pairs)

            def sbank(k):
                gi_, ix_ = k // 4, k % 4
                return PS[2 * (gi_ % 2) + ix_ // 2][:, (ix_ % 2) * 256:(ix_ % 2) * 256 + 256]

            def score_mm(k, KT=KT, h=h):
                j, i = pairs[k]
                Sb = sbank(k)
                mm(Sb.rearrange("p (c n) -> p c n", c=2), KT[:, i * 128:(i + 1) * 128],
                   QKv[:, h % 2, 0:2, j * 128:(j + 1) * 128], True, True)

            def score_diag(k, h=h):
                j, i = pairs[k]
                if i != j:
                    return
                Sb = sbank(k)
                dt_ = af(F_TMP + (j % 4) * 256, 256)
                for c in range(2):
                    tt(dt_[:, c * 128:(c + 1) * 128], Sb[:, c * 128:(c + 1) * 128],
                       af(F_BD + h * 128, 128), ALU.add)

            def score_act(k, h=h):
                j, i = pairs[k]
                Sb = sbank(k)
                pts = PTr[:, k % NPT, :]
                if i == j:
                    act(pts, af(F_TMP + (j % 4) * 256, 256), AF.Exp)
                else:
                    act(pts, Sb, AF.Exp, bias=af(F_CF + h * 16 + (j - i), 1))

            def av(k, h=h, V=V):
                j, i = pairs[k]
                accb = PS[4 + (j % 2)]
                pts = PTr[:, k % NPT, :]
                for c in range(2):
                    mm(accb[:, c * 129:(c + 1) * 129], pts[:, c * 128:(c + 1) * 128], V[:, i, 0:129],
                       i == 0 and c == 0, i == j, skip=True)
                if i != j:
                    return
                r0, r1, r2, ssq, rs2 = stcol(), stcol(), stcol(), stcol(), stcol()
                recip(r0, accb[:, 128:129])
                recip(r1, accb[:, 257:258])
                tt(r2, r1, nlam, ALU.mult)
                o1 = af(F_GE, 128)
                o2 = af(F_GE + 128 + (j % 3) * 128, 128)
                ts(o1, accb[:, 0:128], r0, None, ALU.mult)
                stt(o2, accb[:, 129:257], r2, o1, ALU.mult, ALU.add)
                stt(ONr[:, j % 2, :], o2, 1.0 / 128.0, o2, ALU.mult, ALU.mult, accum=ssq)
                rstd(rs2, ssq, EPS)

                def tail(o2=o2, rs2=rs2, j=j, h=h):
                    on = ONr[:, j % 2, :]
                    stt(on, o2, rs2, SG, ALU.mult, ALU.mult)
                    finish_tile(on, h, j)
                deferred.append((k + 8, tail))

            deferred = []
            GS = 4
            groups = [list(range(a, min(a + GS, n))) for a in range(0, n, GS)]
            NG = len(groups)
            for k in groups[0]:
                score_mm(k)
            for k in groups[0]:
                score_diag(k)
            for gi in range(NG + 1):
                if gi + 1 < NG:
                    for k in groups[gi + 1]:
                        score_mm(k)
                    for k in groups[gi + 1]:
                        score_diag(k)
                if gi < NG:
                    for k in groups[gi]:
                        score_act(k)
                kk = gi * GS
                while deferred and deferred[0][0] <= kk:
                    deferred.pop(0)[1]()
                if dc >= 4 and gi >= 1:
                    for k in groups[gi - 1]:
                        av(k)
                if nxt:
                    nxt.pop(0)()
            while deferred:
                deferred.pop(0)[1]()

    def attn_sb(l):
        g2 = load_gain(l, 2)
        prenorm_seq(list(range(NT)), g2, list(range(NT)))
        zero_q_halves()
        Lbufs = [ab(O_XN, 1024), ab(O_XN + 1024, 1024), AFt[:, F_GE:F_GE + 512].bitcast(BF16)]
        Lbufs = [x.rearrange("p (h n) -> p h n", h=2) for x in Lbufs]
        ATbufs = [ab(O_MISC + k * 1024, 1024).rearrange("p (h n) -> p h n", h=2) for k in range(3)]
        LCall = AFt[:, F_TMP:F_TMP + 2048].bitcast(BF16)
        LCbufs = [LCall[:, k * 1024:(k + 1) * 1024].rearrange("p (h n) -> p h n", h=2) for k in range(4)]
        Wt = PSD[2]
        ACC = PS[6]
        steps = []
        for hp in range(dbg.get("sb_pairs", 8)):
            for g in range(4):
                for i in range(4 * g + 3, -1, -1):
                    steps.append((hp, g, i))
        ctx = {}
        info = {}

        NP = dbg.get("sb_pairs", 8)

        def ensure_proj(hp):
            if (hp, "proj") not in ctx:
                ctx[(hp, "proj")] = project(1, hp, 7)
            chs = ctx[(hp, "proj")][3]
            while chs:
                chs.pop(0)()
            if hp == NP - 1 and "wo" not in ctx:
                ctx["wo"] = True
                load_wo(l)

        def stage_a(n):
            hp, g, i = steps[n]
            ensure_proj(hp)
            QT, KT, V, _ = ctx[(hp, "proj")]
            r = max(0, i - 4 * g)
            c0 = r * 128
            q0 = g * 512 + c0
            q1 = (g + 1) * 512
            diag = i >= 4 * g
            first = (i == 4 * g + 3)
            Zt = PSD[n % 2]
            for hd in range(2):
                mm(Zt[:, hd, c0:512], KT[:, i * 128:(i + 1) * 128], QT[hd][:, q0:q1], True, True)
            act(Zt[:, :, c0:512], Zt[:, :, c0:512], AF.Exp)
            if diag:
                for hd in range(2):
                    tt(Zt[:, hd, c0:c0 + 128], Zt[:, hd, c0:c0 + 128], M01, ALU.mult)
            info[n] = dict(c0=c0, q0=q0, q1=q1, diag=diag, first=first, QT=QT, KT=KT, V=V, Zt=Zt, i=i)

        def stage_a2(n):
            d = info[n]
            c0, first, i, Zt = d["c0"], d["first"], d["i"], d["Zt"]
            L = Lbufs[n % 3]
            act(L[:, :, c0:512], Zt[:, :, c0:512], AF.Ln, bias=1.0)
            lcprev = None if first else info[n - 1]["lcnew"]
            lcnew = None
            if i > 0:
                lcnew = LCbufs[n % 4]
                if lcprev is None:
                    cp(lcnew[:, :, c0:512], L[:, :, c0:512], eng="dve")
                else:
                    tt(lcnew[:, :, c0:512], lcprev[:, :, c0:512], L[:, :, c0:512], ALU.add)
                if c0 > 0:
                    memset(lcnew[:, :, 0:c0], 0.0)
            d.update(lcprev=lcprev, lcnew=lcnew, L=L)

        def stage_bw(n):
            hp, g, i = steps[n]
            d = info[n]
            c0 = d["c0"]
            for hd in range(2):
                seq = [(d["KT"][:, i * 128:(i + 1) * 128], d["QT"][hd][:, d["q0"]:d["q1"]], Wt[:, hd, c0:512]),
                       (NEGTRI, d["L"][:, hd, c0:512], Wt[:, hd, c0:512])]
                if d["lcprev"] is not None:
                    seq.append((NEGONE, d["lcprev"][:, hd, c0:512], Wt[:, hd, c0:512]))
                if d["diag"]:
                    seq.append((IDENT, MDIAG, Wt[:, hd, c0:c0 + 128]))
                for k, (lt, rh, o) in enumerate(seq):
                    mm(o, lt, rh, k == 0, k == len(seq) - 1, skip=True)

        def stage_bexp(n):
            d = info[n]
            c0 = d["c0"]
            AT = ATbufs[n % 3]
            act(AT[:, :, c0:512], Wt[:, :, c0:512], AF.Exp)

        def stage_bav(n):
            hp, g, i = steps[n]
            d = info[n]
            c0 = d["c0"]
            AT = ATbufs[n % 3]
            for hd in range(2):
                mm(ACC[hd * 64:(hd + 1) * 64, c0:512], d["V"][:, i, hd * 64:(hd + 1) * 64], AT[:, hd, c0:512],
                   d["first"], i == 0, skip=True)
            if i == 0:
                cp(OTv[:, hp, g * 512:(g + 1) * 512], ACC, eng="dve")
            del info[n]["L"]

        N = len(steps)

        def a1(k):
            if 0 <= k < N:
                stage_a(k)

        def a2(k):
            if 0 <= k < N:
                stage_a2(k)

        a1(0)
        a2(0)
        a1(1)
        a2(1)
        loc = 0
        for n in range(N + 1):
            if n < N:
                hp = steps[n][0]
                if n == 0 or steps[n - 1][0] != hp:
                    loc = 0
                    if hp + 1 < NP:
                        ctx[(hp + 1, "proj")] = project(1, hp + 1, 7)
                a1(n + 2)
                stage_bw(n)
                stage_bexp(n)
                a2(n + 2)
            if n >= 1:
                stage_bav(n - 1)
            if n < N:
                if hp + 1 < NP and ctx[(hp + 1, "proj")][3]:
                    ctx[(hp + 1, "proj")][3].pop(0)()
                loc += 1

    WOv = ab(O_HT, 8192).rearrange("p (c d) -> p c d", c=8)

    def load_wo(l):
        src = wo_d[l].rearrange("(c p) d -> p c d", p=128)
        dma("pool", WOv[:, 0:4, :], src[:, 0:4, :], ("wo", 0))
        dma("pool", WOv[:, 4:8, :], src[:, 4:8, :], ("wo", 1))

    def wo_phase(l):
        g3 = load_gain(l, 3)
        if "ffn2" in stages:
            wout_a_dma(l * 2 + 1)
        for t in range(NT):
            yb = [(0, 1), (2, 3)][t % 2]
            ys = [PS[yb[0]][:, :], PS[yb[1]][:, :]]
            for hh in range(2):
                for c in range(8):
                    mm(ys[hh], OTv[:, c, t * 128:(t + 1) * 128], WOv[:, c, hh * 512:(hh + 1) * 512], c == 0, c == 7)
            postnorm(t, ys, g3, 1.0)

    def ple(l, store):
        WG = ab(O_OT, 8192).rearrange("p (c d) -> p c d", c=8)
        WP = ab(O_OT + 8192, 2048).rearrange("p (c d) -> p c d", c=2)
        sg = wg_d[l].rearrange("(c p) d -> p c d", p=128)
        dma("pool", WG[:, 0:4, :], sg[:, 0:4, :], ("wg", 0))
        dma("pool", WG[:, 4:8, :], sg[:, 4:8, :], ("wg", 1))
        dma("pool", WP, wp_d[l].rearrange("(c p) d -> p c d", p=128), ("wp", 0))
        g6 = load_gain(l, 6)
        g7 = load_gain(l, 7)
        if l + 1 < depth and "ffn1" in stages:
            wout_a_dma((l + 1) * 2)
        PBr = ab(O_OT + 10240, 1024).rearrange("p (s n) -> p s n", s=4)
        PTTr = ab(O_OT + 11264, 512).rearrange("p (s n) -> p s n", s=2)

        def banks(t):
            return ((0, 1) if t % 2 == 0 else (4, 5)), (2, 3)

        def pload(t):
            if t < NT:
                dma("pool", PBr[:, t % 4, :], p_d[l, t * 128:(t + 1) * 128, :], ("pf", t % 4))

        def front(t):
            pb = PBr[:, t % 4, :]
            pt = ptbank((6, 7))
            for c2 in range(2):
                tr(pt[:, c2 * 128:(c2 + 1) * 128], pb[:, c2 * 128:(c2 + 1) * 128])
            cp(PTTr[:, t % 2, :], pt[:, 0:256], eng="dve")

        def matmuls(t, hslot):
            Gb, Eb = banks(t)
            ptt = PTTr[:, t % 2, :]
            for hh in range(2):
                Gt = PS[Gb[hh]]
                Et = PS[Eb[hh]]
                for c in range(8):
                    mm(Gt, HTv[:, hslot, c, :], WG[:, c, hh * 512:(hh + 1) * 512], c == 0, c == 7)
                for c2 in range(2):
                    mm(Et, ptt[:, c2 * 128:(c2 + 1) * 128], WP[:, c2, hh * 512:(hh + 1) * 512], c2 == 0, c2 == 1)

        def elementwise(t):
            Gb, Eb = banks(t)
            ges = []
            for hh in range(2):
                gs = af(F_BD + hh * 512, 512)
                act(gs, PS[Gb[hh]], AF.Sigmoid)
                ge = af(F_TMP + hh * 512, 512)
                tt(ge, gs, PS[Eb[hh]], ALU.mult)
                ges.append(ge)
            postnorm(t, ges, g7, 1.0, inplace=True)
            if store:
                dma("sp", out_d[t * 128:(t + 1) * 128, :], X[:, t, :], ("x", t))

        for t in range(3):
            pload(t)
        prenorm_seq([0, 1, 2, 3], g6, [0, 1, 2, 3])
        front(0)
        for t in range(NT):
            g, k = t // 4, t % 4
            pload(t + 3)
            if g < 3:
                prenorm_a(4 * (g + 1) + k, g6)
            matmuls(t, (g % 2) * 4 + k)
            if g < 3:
                prenorm_b(4 * (g + 1) + k, ((g + 1) % 2) * 4 + k)
            if t + 1 < NT:
                front(t + 1)
            elementwise(t)

    for t in range(NT):
        dma("sp", X[:, t, :], x_d[t * 128:(t + 1) * 128, :], ("x", t))
    S.add("pool", lambda e: e.memset(NEGHALF, -0.5), r=[], w=[NEGHALF])
    dma("pool", ab(O_CB, 512), cb_d, ("cst", 0))
    dma("sp", af(F_CF, 256), cf_d, ("cst", 1))
    dma("sp", af(F_BD, 1024), bd_d, ("cst", 2))
    dma("sp", LAMB, lam_d.partition_broadcast(128), ("cst", 3))
    dma("sp", SG, sub_d.partition_broadcast(128), ("cst", 4))

    if stages is None:
        stages = ["ffn1", "attn", "wo", "ffn2", "ple"]
    last_stage_store = False
    for l in range(depth):
        final_layer = (l == depth - 1)
        if "ffn1" in stages:
            ffn(l, 0)
        if "attn" in stages:
            if l == 0:
                attn_da(l)
            else:
                attn_sb(l)
            if "wo" in stages:
                wo_phase(l)
        if "ffn2" in stages:
            ffn(l, 1)
        if "ple" in stages:
            ple(l, store=final_layer)
            last_stage_store = final_layer
    if not last_stage_store:
        for t in range(NT):
            dma("sp", out_d[t * 128:(t + 1) * 128, :], X[:, t, :], ("x", t))

    keys = S.finalize()
    if dbg.get('log'):
        S.log = []
        nc._sched = S
    with ExitStack() as es:
        sems = {}
        for i, k in enumerate(keys):
            sems[k] = es.enter_context(nc.semaphore("s%d" % i))
        S.emit(nc, sems)
    return nc


def prepare_inputs(x, p, norm_gains, ffn1_w_in, ffn1_w_out, ffn2_w_in, ffn2_w_out,
                   da_w_qkv, da_w_o, da_lambda, da_subln, sb_w_qkv, sb_w_o, ple_w_proj, ple_w_gate):
    f = lambda a: np.ascontiguousarray(np.asarray(a, dtype=np.float32))
    x, p = f(x), f(p)
    win = np.stack([_lay_win(f(ffn1_w_in)[0]), _lay_win(f(ffn2_w_in)[0]),
                    _lay_win(f(ffn1_w_in)[1]), _lay_win(f(ffn2_w_in)[1])], axis=0)
    wout = np.stack([f(ffn1_w_out)[0], f(ffn2_w_out)[0], f(ffn1_w_out)[1], f(ffn2_w_out)[1]], axis=0)
    wqkv = np.stack([_lay_qkv(f(da_w_qkv)[0]), _lay_qkv(f(sb_w_qkv)[0])], axis=0)
    wo = np.stack([f(da_w_o)[0], f(sb_w_o)[0]], axis=0)
    cf, bd, cb = _const_tables()
    shared = {
        "gains": f(norm_gains), "win": np.ascontiguousarray(win), "wout": np.ascontiguousarray(wout),
        "wqkv": np.ascontiguousarray(wqkv), "wo": np.ascontiguousarray(wo),
        "wg": f(ple_w_gate), "wp": f(ple_w_proj),
        "lam": f(da_lambda).reshape(256), "subln": f(da_subln).reshape(128),
        "cf": cf, "bd": bd, "cb": cb,
    }
    in_maps = []
    for c in range(N_CORES):
        m = dict(shared)
        m["x"] = np.ascontiguousarray(x[c])
        m["p"] = np.ascontiguousarray(p[:, c])
        in_maps.append(m)
    return in_maps


_NC_CACHE = {}


def kernel(**inputs):
    in_maps = prepare_inputs(**inputs)
    if "nc" not in _NC_CACHE:
        _NC_CACHE["nc"] = build_program()
    nc = _NC_CACHE["nc"]
    res = run_bass_kernel_spmd(nc, in_maps, core_ids=list(range(N_CORES)))
    out = np.stack([np.asarray(r["out"], dtype=np.float32) for r in res.results], axis=0)
    return out.reshape(N_CORES, S_LEN, D)
```
